# Optimizing a Trainium2 kernel written in Bass

```python
import math
import jax
import jax.numpy as jnp
from jax import lax
import numpy as np


D_MODEL = 2048
BATCH = 1
SEQ = 8192
DEPTH = 4

GRID_W = 64
CTX_LEN = 256
D_MIX = D_MODEL
NORM_EPS = 1e-6

D_A = D_MIX // 4
A_HEADS = 4
A_HEAD_DIM = D_A // A_HEADS
A_CHUNK = 128

D_B = D_MIX // 4
B_HEADS = 4
B_KEY_DIM = D_B // B_HEADS
B_VAL_DIM = D_B // B_HEADS
B_CHUNK = 64

D_C = D_MIX // 2
C_HEADS = 4
C_HEAD_DIM = D_C // C_HEADS // 2
C_VAL_DIM = 2 * C_HEAD_DIM
Q_BLOCK = 128
ROPE_THETA = 10000.0
ROPE_AXIS_DIM = C_HEAD_DIM // 2

D_IN = 3 * D_A + 5 * D_B + 4 * D_C

kernel_name = 'hybrid_gmlp_hgrn2_diffattn_dit_trunk'


def rms_norm(x, w):
    xf = x.astype(jnp.float32)
    y = xf * lax.rsqrt(jnp.mean(xf * xf, axis=-1, keepdims=True) + NORM_EPS)
    return (y * w.astype(jnp.float32)).astype(x.dtype)


def layer_norm(x, w, b):
    xf = x.astype(jnp.float32)
    mu = jnp.mean(xf, axis=-1, keepdims=True)
    xc = xf - mu
    var = jnp.mean(xc * xc, axis=-1, keepdims=True)
    return (xc * lax.rsqrt(var + NORM_EPS) * w.astype(jnp.float32) + b.astype(jnp.float32)).astype(x.dtype)


def split_columns(p):
    sizes = (D_A, D_A, D_A, D_B, D_B, D_B, D_B, D_B, D_C, D_C, D_C, D_C)
    idx = np.cumsum(sizes)[:-1].tolist()
    return jnp.split(p, idx, axis=-1)


def axial_rope_tables(seq):
    rows = seq // GRID_W
    row_ids = jnp.repeat(jnp.arange(rows, dtype=jnp.float32), GRID_W)
    col_ids = jnp.tile(jnp.arange(GRID_W, dtype=jnp.float32), rows)
    inv = ROPE_THETA ** (-jnp.arange(0, ROPE_AXIS_DIM, 2, dtype=jnp.float32) / ROPE_AXIS_DIM)
    ang_r = row_ids[:, None] * inv[None, :]
    ang_c = col_ids[:, None] * inv[None, :]
    return (jnp.cos(ang_r), jnp.sin(ang_r), jnp.cos(ang_c), jnp.sin(ang_c))


def rope_rotate(x, cos, sin):
    x1, x2 = jnp.split(x.astype(jnp.float32), 2, axis=-1)
    cos = cos[:, None, None, :]
    sin = sin[:, None, None, :]
    return jnp.concatenate([x1 * cos - x2 * sin, x2 * cos + x1 * sin], axis=-1)


def rope_2d(x, tabs):
    cos_r, sin_r, cos_c, sin_c = tabs
    x_row, x_col = jnp.split(x, 2, axis=-1)
    y = jnp.concatenate([rope_rotate(x_row, cos_r, sin_r), rope_rotate(x_col, cos_c, sin_c)], axis=-1)
    return y.astype(x.dtype)


def gmlp_branch(u, v, z, ln_w, ln_b, ws, bs):
    bsz, t, _ = v.shape
    u = jax.nn.gelu(u, approximate=False)
    v = layer_norm(jax.nn.gelu(v, approximate=False), ln_w, ln_b)
    vr = v.reshape(bsz, t // A_CHUNK, A_CHUNK, A_HEADS, A_HEAD_DIM)
    s = jnp.einsum('hts,bnshd->bnthd', ws, vr) + bs.T[None, None, :, :, None]
    s = s.reshape(bsz, t, D_A)
    return u * s * jax.nn.silu(z)


def hgrn_heads(a):
    bsz, t, _ = a.shape
    return a.reshape(bsz, t, B_HEADS, a.shape[-1] // B_HEADS)


def hgrn_gates(a, lb):
    af = a.astype(jnp.float32)
    logf = jnp.logaddexp(jnp.log(lb), jnp.log1p(-lb) + jax.nn.log_sigmoid(af))
    k = (1.0 - lb) * jax.nn.sigmoid(-af)
    return hgrn_heads(k), hgrn_heads(logf)


def hgrn2_chunk_scan(q, k, v, logf, state0):
    bsz, t, h, _ = q.shape
    dv = v.shape[-1]
    n = t // B_CHUNK

    def to_chunks(a):
        return a.astype(jnp.float32).reshape(bsz, n, B_CHUNK, h, a.shape[-1]).transpose(1, 0, 2, 3, 4)

    lower = jnp.tril(jnp.ones((B_CHUNK, B_CHUNK), dtype=bool))[None, :, :, None, None]

    def step(state, inp):
        qc, kc, vc, gc = inp
        b = jnp.cumsum(gc, axis=1)
        o_inter = jnp.einsum('bthk,bhkv->bthv', qc * jnp.exp(b), state)
        decay = jnp.exp(jnp.where(lower, b[:, :, None] - b[:, None, :], -jnp.inf))
        scores = jnp.einsum('bthk,bshk,btshk->bths', qc, kc, decay)
        o_intra = jnp.einsum('bths,bshv->bthv', scores, vc)
        b_last = b[:, -1]
        k_dec = kc * jnp.exp(b_last[:, None] - b)
        new_state = jnp.exp(b_last)[..., None] * state + jnp.einsum('bshk,bshv->bhkv', k_dec, vc)
        return new_state, o_inter + o_intra

    final, out = lax.scan(step, state0, (to_chunks(q), to_chunks(k), to_chunks(v), to_chunks(logf)))
    out = out.transpose(1, 0, 2, 3, 4).reshape(bsz, t, h, dv)
    return final, out


def hgrn_output(o, z, norm_w):
    bsz, t = o.shape[0], o.shape[1]
    o = rms_norm(o.astype(z.dtype), norm_w).reshape(bsz, t, D_B)
    return o * jax.nn.silu(z)


def diff_attend(q, k, v, lam):
    s = jnp.einsum('bqhmd,bkhmd->bhmqk', q * (C_HEAD_DIM ** -0.5), k).astype(jnp.float32)
    p = jax.nn.softmax(s, axis=-1)
    w = p[:, :, 0] - lam * p[:, :, 1]
    return jnp.einsum('bhqk,bkhe->bqhe', w.astype(v.dtype), v)


def diff_output(o, z, subln_w, lam_init):
    bsz, t = o.shape[0], o.shape[1]
    o = (rms_norm(o, subln_w) * (1.0 - lam_init)).reshape(bsz, t, D_C)
    return o * jax.nn.silu(z)


def hybrid_layer(x, xc, c, c_ctx, ada_w, ada_b, norm_w, w_in, g_ln_w, g_ln_b, g_ws, g_bs,
                 lb_fwd, lb_bwd, h_norm_w, lam_params, lam_init, subln_w, w_out, rope, ctx_out):
    bsz, t, _ = x.shape
    shift, scale, gate = jnp.split(jax.nn.silu(c) @ ada_w + ada_b, 3, axis=-1)
    shift_c, scale_c, gate_c = jnp.split(jax.nn.silu(c_ctx) @ ada_w + ada_b, 3, axis=-1)
    h = rms_norm(x, norm_w) * (1.0 + scale[:, None]) + shift[:, None]
    hc = rms_norm(xc, norm_w) * (1.0 + scale_c) + shift_c

    (a_u, a_v, a_z, b_q, b_i, b_ff, b_fb, b_z, c_q, c_k, c_v, c_z) = split_columns(h @ w_in)
    (ac_u, ac_v, ac_z, bc_q, bc_i, bc_ff, bc_fb, bc_z, cc_q, cc_k, cc_v, cc_z) = split_columns(hc @ w_in)

    y_a = gmlp_branch(a_u, a_v, a_z, g_ln_w, g_ln_b, g_ws, g_bs)

    qx, vx = hgrn_heads(jax.nn.silu(b_q)), hgrn_heads(b_i)
    kfx, gfx = hgrn_gates(b_ff, lb_fwd)
    kbx, gbx = hgrn_gates(b_fb, lb_bwd)
    qcx, vcx = hgrn_heads(jax.nn.silu(bc_q)), hgrn_heads(bc_i)
    kfc, gfc = hgrn_gates(bc_ff, lb_fwd)
    kbc, gbc = hgrn_gates(bc_fb, lb_bwd)
    flip = lambda a: jnp.flip(a, axis=1)
    s0 = jnp.zeros((bsz, B_HEADS, B_KEY_DIM, B_VAL_DIM), jnp.float32)
    s_fwd, oc_f = hgrn2_chunk_scan(qcx, kfc, vcx, gfc, s0)
    _, ox_f = hgrn2_chunk_scan(qx, kfx, vx, gfx, s_fwd)
    s_bwd, oc_b = hgrn2_chunk_scan(flip(qcx), flip(kbc), flip(vcx), flip(gbc), s0)
    _, ox_b = hgrn2_chunk_scan(flip(qx), flip(kbx), flip(vx), flip(gbx), s_bwd)
    y_b = hgrn_output(ox_f + flip(ox_b), b_z, h_norm_w)

    lp = lam_params.astype(jnp.float32)
    lam = jnp.exp(jnp.sum(lp[0] * lp[1])) - jnp.exp(jnp.sum(lp[2] * lp[3])) + lam_init
    n_ctx = xc.shape[1]
    q_lat = rope_2d(c_q.reshape(bsz, t, C_HEADS, 2, C_HEAD_DIM), rope)
    k_lat = rope_2d(c_k.reshape(bsz, t, C_HEADS, 2, C_HEAD_DIM), rope)
    v_lat = c_v.reshape(bsz, t, C_HEADS, C_VAL_DIM)
    q_ctx = cc_q.reshape(bsz, n_ctx, C_HEADS, 2, C_HEAD_DIM)
    k_ctx = cc_k.reshape(bsz, n_ctx, C_HEADS, 2, C_HEAD_DIM)
    v_ctx = cc_v.reshape(bsz, n_ctx, C_HEADS, C_VAL_DIM)
    keys = jnp.concatenate([k_lat, k_ctx], axis=1)
    vals = jnp.concatenate([v_lat, v_ctx], axis=1)
    nb = t // Q_BLOCK
    q_blocks = q_lat.reshape(bsz, nb, Q_BLOCK, C_HEADS, 2, C_HEAD_DIM).transpose(1, 0, 2, 3, 4, 5)
    o_blocks = lax.map(lambda qb: diff_attend(qb, keys, vals, lam), q_blocks)
    o_lat = o_blocks.transpose(1, 0, 2, 3, 4).reshape(bsz, t, C_HEADS, C_VAL_DIM)
    y_c = diff_output(o_lat, c_z, subln_w, lam_init)

    y = jnp.concatenate([y_a, y_b, y_c], axis=-1) @ w_out
    x_new = x + gate[:, None] * y

    if ctx_out:
        yc_a = gmlp_branch(ac_u, ac_v, ac_z, g_ln_w, g_ln_b, g_ws, g_bs)
        yc_b = hgrn_output(oc_f + flip(oc_b), bc_z, h_norm_w)
        yc_c = diff_output(diff_attend(q_ctx, k_ctx, v_ctx, lam), cc_z, subln_w, lam_init)
        yc = jnp.concatenate([yc_a, yc_b, yc_c], axis=-1) @ w_out
        xc = xc + gate_c * yc
    return x_new, xc


def setup_inputs(seed: int = 0) -> dict:
    key = jax.random.key(seed)
    ks = jax.random.split(key, 20)
    f32 = jnp.float32

    def nrm(k, shape, s):
        return jax.random.normal(k, shape, f32) * s

    return {
        'x': nrm(ks[0], (BATCH, SEQ, D_MODEL), 1.0),
        'c': nrm(ks[1], (BATCH, D_MODEL), 1.0),
        'ctx': nrm(ks[2], (BATCH, CTX_LEN, D_MODEL), 1.0),
        'c_ctx': nrm(ks[3], (D_MODEL,), 1.0),
        'ada_w': nrm(ks[4], (DEPTH, D_MODEL, 3 * D_MODEL), 0.5 * D_MODEL ** -0.5),
        'ada_b': nrm(ks[5], (DEPTH, 3 * D_MODEL), 0.02),
        'norm_w': 1.0 + nrm(ks[6], (DEPTH, D_MODEL), 0.02),
        'w_in': nrm(ks[7], (DEPTH, D_MODEL, D_IN), D_MODEL ** -0.5),
        'gmlp_ln_w': 1.0 + nrm(ks[8], (DEPTH, D_A), 0.02),
        'gmlp_ln_b': nrm(ks[9], (DEPTH, D_A), 0.02),
        'gmlp_ws': nrm(ks[10], (DEPTH, A_HEADS, A_CHUNK, A_CHUNK), A_CHUNK ** -0.5),
        'gmlp_bs': 1.0 + nrm(ks[11], (DEPTH, A_HEADS, A_CHUNK), 0.02),
        'hgrn_lower_bounds': nrm(ks[12], (2, DEPTH, D_B), 0.1),
        'hgrn_norm_w': 1.0 + nrm(ks[13], (DEPTH, B_VAL_DIM), 0.02),
        'diff_lambda': nrm(ks[14], (DEPTH, 4, C_HEAD_DIM), 0.1),
        'diff_subln_w': 1.0 + nrm(ks[15], (DEPTH, C_VAL_DIM), 0.02),
        'w_out': nrm(ks[16], (DEPTH, D_MIX, D_MODEL), D_MIX ** -0.5),
        'final_norm_w': 1.0 + nrm(ks[17], (D_MODEL,), 0.02),
    }


def reference(x, c, ctx, c_ctx, ada_w, ada_b, norm_w, w_in, gmlp_ln_w, gmlp_ln_b, gmlp_ws, gmlp_bs,
              hgrn_lower_bounds, hgrn_norm_w, diff_lambda, diff_subln_w, w_out, final_norm_w):
    rope = axial_rope_tables(x.shape[1])
    lb_soft = jax.nn.softmax(hgrn_lower_bounds.astype(jnp.float32), axis=1)
    lb_cum = jnp.cumsum(lb_soft, axis=1)
    lb_all = lb_cum - lb_cum[:, :1]
    xc = ctx
    for layer in range(DEPTH):
        lam_init = 0.8 - 0.6 * math.exp(-0.3 * layer)
        x, xc = hybrid_layer(x, xc, c, c_ctx, ada_w[layer], ada_b[layer], norm_w[layer], w_in[layer],
                             gmlp_ln_w[layer], gmlp_ln_b[layer], gmlp_ws[layer], gmlp_bs[layer],
                             lb_all[0, layer], lb_all[1, layer], hgrn_norm_w[layer], diff_lambda[layer],
                             lam_init, diff_subln_w[layer], w_out[layer], rope, layer < DEPTH - 1)
    return rms_norm(x, final_norm_w)
```

```python
import bisect
import os
import math
from contextlib import ExitStack

import numpy as np
import concourse.bass as bass
import concourse.mybir as mybir
from concourse.bass_utils import run_bass_kernel_spmd

F32 = mybir.dt.float32
BF16 = mybir.dt.bfloat16
AF = mybir.ActivationFunctionType
ALU = mybir.AluOpType

NCORES = 8
D = 2048
LAT = 1024
CTX = 256
TOK = LAT + CTX
NT = TOK // 128
DEPTH = 4
EPS = 1e-6
GROUP_ORDER = [2, 0, 1, 7, 8, 9, 10, 11, 12, 13, 14, 15, 3, 4, 5, 6]


class SemGrp:
    def __init__(self, name):
        self.name = name
        self.sem = None
        self.ids = []
        self.waiters = []
        self.inc = 16


class Tok:
    def __init__(self, name, grp=None):
        self.name = name
        self.w = []
        self.r = []
        self.grp = grp


class Op:
    __slots__ = ("id", "eng", "fn", "deps", "dma", "grp", "sig", "sigidx", "epoch")


class Prog:
    ENGS = ("tensor", "vector", "scalar", "gpsimd", "sync")

    def __init__(self, nc, es):
        self.nc = nc
        self.es = es
        self.ops = []
        self.grps = []
        self.grp_by_name = {}
        self.local_idx = 0
        self.local_gidx = 0
        self.last = {}
        self.epoch = 0
        self.esems = {}
        self._mk_esem()

    CE = ("tensor", "vector", "scalar", "gpsimd")

    def _mk_esem(self):
        self.esems[self.epoch] = {e: self.es.enter_context(self.nc.semaphore(f"sem_{e}_{self.epoch}")) for e in self.CE}

    def new_epoch(self):
        self.epoch += 1
        self._mk_esem()

    def tok(self, name, grp=None):
        return Tok(name, grp)

    def grp(self, name, inc=16):
        g = self.grp_by_name.get(name)
        if g is None:
            g = SemGrp(name)
            g.inc = inc
            self.grp_by_name[name] = g
        return g

    def cc(self, fn, reads, writes):
        tok = writes[0]
        if tok.grp is None:
            tok.grp = self.grp("cc_" + tok.name, 1)
        return self._add("gpsimd", fn, reads, writes, [], dma_tok=tok)

    def _add(self, eng, fn, reads, writes, partial, dma_tok=None, force_sig=False):
        deps = set()
        for t in reads:
            deps.update(t.w)
        for t in list(writes) + list(partial):
            deps.update(t.w)
            deps.update(t.r)
        op = Op()
        op.id = len(self.ops)
        op.eng = eng
        op.fn = fn
        op.epoch = self.epoch
        op.dma = dma_tok is not None
        op.grp = None
        op.sig = force_sig
        op.sigidx = 0
        if op.dma:
            if dma_tok.grp is None:
                if eng == "gpsimd":
                    dma_tok.grp = self.grp(f"locg{self.local_gidx}")
                    self.local_gidx += 1
                else:
                    dma_tok.grp = self.grp(f"loc{self.local_idx}")
                    self.local_idx += 1
            g = dma_tok.grp
            if g.sem is None:
                g.sem = self.es.enter_context(self.nc.semaphore("d_" + g.name))
                self.grps.append(g)
            deps.update(g.waiters)
            g.waiters = []
            g.ids.append(op.id)
            op.grp = g
        latest = {}
        pruned = set()
        for d_ in deps:
            dep = self.ops[d_]
            if dep.dma:
                pruned.add(d_)
            elif latest.get(dep.eng, -1) < d_:
                latest[dep.eng] = d_
        pruned.update(latest.values())
        deps = pruned
        op.deps = deps
        for d in deps:
            dep = self.ops[d]
            if dep.dma:
                dep.grp.waiters.append(op.id)
        self.ops.append(op)
        if fn is not None:
            self.last[eng if not op.dma else ("dma", id(op.grp))] = op.id
        for t in reads:
            t.r.append(op.id)
        for t in writes:
            t.w = [op.id]
            t.r = []
        for t in partial:
            t.w.append(op.id)
            t.r = []
        return op.id

    def I(self, eng, meth, reads, writes, partial=(), **kw):
        nc = self.nc
        return self._add(eng, lambda: getattr(getattr(nc, eng), meth)(**kw), reads, writes, partial)

    def dma(self, queue, out, in_, reads, writes, partial=(), **kw):
        nc = self.nc
        tok = (list(writes) + list(partial))[0]
        return self._add(queue, lambda: getattr(nc, queue).dma_start(out=out, in_=in_, **kw), reads, writes, partial, dma_tok=tok)

    def barrier(self):
        deps = set(self.last.values())
        for e in self.ENGS:
            op = Op()
            op.id = len(self.ops)
            op.eng = e
            op.fn = None
            op.epoch = self.epoch
            op.dma = False
            op.grp = None
            op.sig = False
            op.sigidx = 0
            op.deps = set(deps)
            self.ops.append(op)
        for g in self.grps:
            g.waiters = []
        self.local_idx = 0
        self.local_gidx = 0

    def emit(self):
        nc = self.nc
        ops = self.ops
        for op in ops:
            for d in op.deps:
                dep = ops[d]
                if not dep.dma and not (dep.eng == op.eng and dep.eng == "tensor"):
                    dep.sig = True
        sigcount = {}
        waited = {e: {} for e in self.ENGS}
        for op in ops:
            E = getattr(nc, op.eng)
            need = {}
            for d in op.deps:
                dep = ops[d]
                if dep.dma:
                    g = dep.grp
                    val = g.inc * bisect.bisect_left(g.ids, op.id)
                    key = ("d", id(g))
                    sem = g.sem
                else:
                    if dep.eng == op.eng and op.eng == "tensor":
                        continue
                    if dep.epoch < op.epoch:
                        continue
                    val = dep.sigidx
                    key = ("e", dep.eng, dep.epoch)
                    sem = self.esems[dep.epoch][dep.eng]
                if need.get(key, (None, 0))[1] < val:
                    need[key] = (sem, val)
            for key, (sem, val) in need.items():
                if waited[op.eng].get(key, 0) >= val:
                    continue
                E.wait_ge(sem, val)
                waited[op.eng][key] = val
            if op.fn is None:
                continue
            ins = op.fn()
            if op.dma:
                ins.then_inc(op.grp.sem, op.grp.inc)
            elif op.sig:
                k_ = (op.eng, op.epoch)
                sigcount[k_] = sigcount.get(k_, 0) + 1
                op.sigidx = sigcount[k_]
                ins.then_inc(self.esems[op.epoch][op.eng], 1)
        print("max engine sem counts", {k: v for k, v in sigcount.items() if v > 1000})


class Arena:
    def __init__(self, ap):
        self.ap = ap
        self.n = ap.shape[1]
        self.off = 0

    def reset(self):
        self.off = 0

    def f32(self, n):
        assert self.off + n <= self.n, ("arena overflow", self.off, n, self.n)
        a = self.ap[:, self.off:self.off + n]
        self.off += n
        return a

    def bf16(self, n):
        w = (n + 1) // 2
        assert self.off + w <= self.n, ("arena overflow", self.off, w, self.n)
        a = self.ap[:, self.off:self.off + w].bitcast(BF16)
        self.off += w
        return a


def build(n_layers=DEPTH, debug=False):
    nc = bass.Bass("TRN2", target_bir_lowering=False)
    es = ExitStack()
    P = Prog(nc, es)

    def din(name, shape, dt=F32):
        return nc.dram_tensor(name, list(shape), dt, kind="ExternalInput").ap()

    def dscr(name, shape, dt=F32, out=False):
        if out:
            return nc.dram_tensor(name, list(shape), dt, kind="ExternalOutput").ap()
        return nc.dram_tensor(name, list(shape), dt).ap()

    x_in = din("x", [LAT, D])
    ctx_in = din("ctx", [CTX, D])
    cc_in = din("cc", [128, 32])
    NLW = n_layers if debug else DEPTH
    adaw_in = din("ada_w", [NLW, D, 768])
    adab_in = din("ada_b", [NLW * 768])
    normw_in = din("norm_w", [DEPTH, D])
    win_in = din("w_in", [NLW, D, 8192])
    wout_in = din("w_out", [NLW, D, D])
    glnw_in = din("gmlp_ln_w", [DEPTH, 512])
    glnb_in = din("gmlp_ln_b", [DEPTH, 512])
    gwsT_in = din("gmlp_wsT", [DEPTH, 128, 512])
    gbs_in = din("gmlp_bs", [128, DEPTH * 4])
    hlb_in = din("hgrn_lb", [2 * DEPTH * 512])
    hnw_in = din("hgrn_norm_w", [DEPTH, 128])
    dlam_in = din("diff_lambda", [DEPTH, 512])
    dsub_in = din("diff_subln_w", [DEPTH, 256])
    fnw_in = din("final_norm_w", [D])
    cmat_in = din("cmat", [128, 9 * 128])
    rope_in = din("rope", [128, 2 * LAT])
    oneh_in = din("onehot", [128, 8])
    out_ap = nc.dram_tensor("out", [LAT, D], F32, kind="ExternalOutput").ap()

    modloc = dscr("modloc", [NLW * 6, 256])
    modgat = dscr("modgat", [NCORES * NLW * 6, 256])
    YA = dscr("YA", [TOK, 512], BF16, out=bool(debug))
    YB = dscr("YB", [TOK, 512], BF16, out=bool(debug))
    YC = dscr("YC", [TOK, 1024], BF16, out=bool(debug))
    ZB = dscr("ZB", [TOK, 512], BF16)
    ZC = dscr("ZC", [TOK, 1024], BF16)
    HQ = dscr("HQ", [TOK, 512], BF16)
    HV = dscr("HV", [TOK, 512], BF16)
    HG = dscr("HG", [2, TOK, 512], F32, out=bool(debug))
    HK = dscr("HK", [2, TOK, 512], BF16)
    OFB = dscr("OFB", [2, TOK, 512], F32, out=bool(debug))
    QTs = dscr("QTs", [8 * 128, TOK], BF16)
    KTloc = dscr("KTloc", [8 * 128, LAT], BF16)
    KTc = dscr("KTc", [8 * 128, CTX], BF16)
    Vloc = dscr("Vloc", [LAT, 1032], BF16)
    Vcx = dscr("Vcx", [CTX, 1032], BF16)
    KTgat = dscr("KTgat", [NCORES * 1024, LAT], BF16)
    Vgat = dscr("Vgat", [NCORES * LAT, 1032], BF16)
    QHAT = dscr("QHAT", [2 * 8 * 128, 512], BF16)
    HSloc = dscr("HSloc", [256, 516])
    HSgat = dscr("HSgat", [NCORES * 256, 516])
    SCTX = dscr("SCTX", [256, 512])
    XS = dscr("XS", [2, TOK, D])

    cm = es.enter_context(nc.sbuf_tensor("cm_sb", [128, 9 * 128], BF16))
    rope = es.enter_context(nc.sbuf_tensor("rope_sb", [128, 2 * LAT], F32))
    oneh = es.enter_context(nc.sbuf_tensor("oneh_sb", [128, 8], F32))
    small = es.enter_context(nc.sbuf_tensor("small_sb", [128, 256], F32))
    arena_t = es.enter_context(nc.sbuf_tensor("arena_sb", [128, 48000], F32))
    A = Arena(arena_t[:, :])
    psT = [es.enter_context(nc.psum_tensor(f"psT{i}", [128, 1024], BF16)) for i in range(2)]
    ps = [es.enter_context(nc.psum_tensor(f"ps{i}", [128, 512], F32)) for i in range(6)]
    psT_t = [P.tok(f"psT{i}") for i in range(2)]
    ps_t = [P.tok(f"ps{i}") for i in range(6)]

    ident = cm[:, 0:128]
    Mq = [cm[:, 128:256], cm[:, 512:640]]
    Mdec = [cm[:, 256:384], cm[:, 640:768]]
    Mmask = [cm[:, 384:512], cm[:, 768:896]]
    perm = cm[:, 896:1024]
    selv = [cm[:, 1024:1032], cm[:, 1032:1040]]
    t_const = P.tok("const", P.grp("const"))

    P.dma("gpsimd", cm[:, :], cmat_in, [], [t_const])
    t_rope = P.tok("rope", P.grp("rope"))
    P.dma("sync", rope[:, :], rope_in, [], [t_rope])
    rmask = small[:, 64:68]
    t_rmask = P.tok("rmask", P.grp("rmask"))
    P.dma("sync", rmask, cmat_in[:, 1024 + 4:1024 + 8], [], [t_rmask])
    t_oneh = P.tok("oneh", P.grp("oneh"))
    P.dma("sync", oneh[:, :], oneh_in, [], [t_oneh])

    A.reset()
    cc = A.f32(32)
    t_cc = P.tok("cc")
    P.dma("sync", cc, cc_in, [], [t_cc])
    scc = A.f32(32)
    t_scc = P.tok("scc")
    P.I("scalar", "activation", [t_cc], [t_scc], out=scc, in_=cc, func=AF.Silu)
    adab = A.f32(NLW * 768)
    t_adab = P.tok("adab")
    P.dma("sync", adab[0:2, :], adab_in.partition_broadcast(2), [], [t_adab])
    wa = [A.f32(768) for _ in range(3)]
    wa_t = [P.tok(f"wa{i}") for i in range(3)]
    mo = [A.f32(768) for _ in range(2)]
    mo_t = [P.tok(f"mo{i}") for i in range(2)]
    t_modloc = P.tok("modloc", P.grp("modloc"))
    sccv = scc.rearrange("p (k s) -> p k s", s=2)
    n = 0
    for l in range(NLW):
        pA, pB = ps[0], ps[1]
        for kc in range(16):
            sl = n % 3
            n += 1
            P.dma("sync", wa[sl], adaw_in[l, kc * 128:(kc + 1) * 128, :], [], [wa_t[sl]])
            P.I("tensor", "matmul", [wa_t[sl], t_scc], [], partial=[ps_t[0]], out=pA[0:2, 0:512], lhsT=sccv[:, kc, :], rhs=wa[sl][:, 0:512], start=(kc == 0), stop=(kc == 15))
            P.I("tensor", "matmul", [wa_t[sl], t_scc], [], partial=[ps_t[1]], out=pB[0:2, 0:256], lhsT=sccv[:, kc, :], rhs=wa[sl][:, 512:768], start=(kc == 0), stop=(kc == 15))
        m = mo[l % 2]
        mt = mo_t[l % 2]
        P.I("vector", "tensor_tensor", [ps_t[0], t_adab], [mt], out=m[0:2, 0:512], in0=pA[0:2, 0:512], in1=adab[0:2, l * 768:l * 768 + 512], op=ALU.add)
        P.I("vector", "tensor_tensor", [ps_t[1], t_adab], [], partial=[mt], out=m[0:2, 512:768], in0=pB[0:2, 0:256], in1=adab[0:2, l * 768 + 512:(l + 1) * 768], op=ALU.add)
        dst = modloc.rearrange("(l s b) c -> l s b c", s=2, b=3)[l]
        P.dma("sync", dst, m[0:2, :].rearrange("s (b c) -> s b c", c=256), [mt], [], partial=[t_modloc])
    t_modgat = P.tok("modgat")
    P.cc(lambda: nc.gpsimd.collective_compute("AllGather", ALU.bypass, replica_groups=[list(range(NCORES))], ins=[modloc.opt()], outs=[modgat.opt()]), [t_modloc], [t_modgat])
    P.barrier()
    P.new_epoch()
    if debug == "0":
        dbg0 = nc.dram_tensor("dbg0", [NCORES * NLW * 6, 256], F32, kind="ExternalOutput").ap()
        t_dbg = P.tok("dbg0")
        P.dma("sync", dbg0, modgat, [t_modgat], [t_dbg])
        n_layers = 0

    def modvec(l, s, b):
        return modgat.rearrange("(r q) c -> q r c", q=NLW * 6)[(l * 2 + s) * 3 + b]

    def bcast(ap2d):
        return ap2d.partition_broadcast(128)

    for l in range(n_layers):
        xs_src = None if l == 0 else XS[(l % 2)]
        xs_dst = XS[((l + 1) % 2)]
        lam_init = 0.8 - 0.6 * math.exp(-0.3 * l)

        def xrows(t):
            if l == 0:
                return x_in[t * 128:(t + 1) * 128, :] if t < 8 else ctx_in[(t - 8) * 128:(t - 7) * 128, :]
            return xs_src[t * 128:(t + 1) * 128, :]

        A.reset()
        hT = A.bf16(16 * TOK).rearrange("p (k t) -> p k t", t=TOK)
        t_hT = [P.tok(f"hT{t}") for t in range(NT)]
        Wb = [A.bf16(16 * 512).rearrange("p (k n) -> p k n", n=512) for _ in range(2)]
        Wb_t = [P.tok(f"W{i}") for i in range(2)]
        G = A.f32(D)
        Sh = A.f32(D)
        t_G = P.tok("G")
        t_Sh = P.tok("Sh")
        nwb = A.f32(D)
        t_nwb = P.tok("nwb")
        xt = [A.f32(D) for _ in range(2)]
        xt_t = [P.tok(f"xt{i}") for i in range(2)]
        xn = [A.bf16(D) for _ in range(2)]
        xn_t = [P.tok(f"xn{i}") for i in range(2)]
        st = small
        t_st = [P.tok(f"st{i}") for i in range(2)]

        P.dma("sync", nwb, bcast(normw_in[l:l + 1, :]).rearrange("p o d -> p (o d)") if False else normw_in[l, :].partition_broadcast(128), [], [t_nwb])
        for t in range(NT):
            if t == 0 or t == 8:
                s = 0 if t == 0 else 1
                P.dma("sync", G.rearrange("p (r c) -> p r c", c=256), modvec(l, s, 1).partition_broadcast(128), [t_modgat], [t_G])
                P.dma("sync", Sh.rearrange("p (r c) -> p r c", c=256), modvec(l, s, 0).partition_broadcast(128), [t_modgat], [t_Sh])
                P.I("vector", "scalar_tensor_tensor", [t_G, t_nwb], [t_G], out=G, in0=G, scalar=1.0, in1=nwb, op0=ALU.add, op1=ALU.mult)
            b = t % 2
            P.dma("sync", xt[b], xrows(t), [], [xt_t[b]])
            ss = st[:, b * 4:b * 4 + 1]
            rs = st[:, b * 4 + 1:b * 4 + 2]
            P.I("scalar", "activation", [xt_t[b]], [xn_t[b], t_st[b]], out=xn[b], in_=xt[b], func=AF.Square, accum_out=ss)
            P.I("scalar", "activation", [t_st[b]], [t_st[b]], out=rs, in_=ss, func=AF.Sqrt, scale=1.0 / D, bias=EPS)
            P.I("vector", "reciprocal", [t_st[b]], [t_st[b]], out=rs, in_=rs)
            P.I("vector", "scalar_tensor_tensor", [xt_t[b], t_st[b], t_G], [xt_t[b]], out=xt[b], in0=xt[b], scalar=rs, in1=G, op0=ALU.mult, op1=ALU.mult)
            P.I("gpsimd", "tensor_tensor", [xt_t[b], t_Sh], [xn_t[b]], out=xn[b], in0=xt[b], in1=Sh, op=ALU.add)
            for half in range(2):
                pt = psT[half]
                for j in range(8):
                    kc = half * 8 + j
                    P.I("tensor", "transpose", [xn_t[b], t_const], [], partial=[psT_t[half]], out=pt[:, j * 128:(j + 1) * 128], in_=xn[b][:, kc * 128:(kc + 1) * 128], identity=ident)
                dsth = hT[:, half * 8:(half + 1) * 8, t * 128:(t + 1) * 128]
                srch = pt[:, :].rearrange("p (k t) -> p k t", t=128)
                if half == 0:
                    P.I("vector", "tensor_copy", [psT_t[half]], [], partial=[t_hT[t]], out=dsth, in_=srch)
                else:
                    P.I("scalar", "copy", [psT_t[half]], [], partial=[t_hT[t]], out=dsth, in_=srch)

        if debug == "A1":
            dbgh = nc.dram_tensor("dbgh", [128, 16 * TOK], BF16, kind="ExternalOutput").ap()
            t_dbgh = P.tok("dbgh")
            P.dma("sync", dbgh, hT.rearrange("p k t -> p (k t)"), t_hT, [t_dbgh])
            break
        lnw = A.f32(512)
        lnb = A.f32(512)
        t_ln = P.tok("ln")
        P.dma("sync", lnw, glnw_in[l, :].partition_broadcast(128), [], [t_ln])
        P.dma("sync", lnb, glnb_in[l, :].partition_broadcast(128), [], [], partial=[t_ln])
        wsT = A.bf16(512)
        t_wsT = P.tok("wsT")
        P.dma("gpsimd", wsT, gwsT_in[l], [], [t_wsT])
        gbs = A.f32(DEPTH * 4)
        t_gbs = P.tok("gbs")
        P.dma("sync", gbs, gbs_in, [], [t_gbs])
        lbraw = A.f32(2 * DEPTH * 512)
        t_lb = P.tok("lb")
        P.dma("sync", lbraw, hlb_in.partition_broadcast(128), [], [t_lb])
        lbe = lbraw.rearrange("p (d l c) -> p d l c", d=2, l=DEPTH)
        P.I("scalar", "activation", [t_lb], [t_lb], out=lbraw, in_=lbraw, func=AF.Exp)
        lbs = A.f32(1024).rearrange("p (d c) -> p d c", d=2)
        lbv = A.f32(1024).rearrange("p (d c) -> p d c", d=2)
        oml = A.f32(1024).rearrange("p (d c) -> p d c", d=2)
        t_lbv = P.tok("lbv")
        P.I("vector", "tensor_tensor", [t_lb], [t_lbv], out=lbs, in0=lbe[:, :, 0, :], in1=lbe[:, :, 1, :], op=ALU.add)
        P.I("vector", "tensor_tensor", [t_lb, t_lbv], [t_lbv], out=lbs, in0=lbs, in1=lbe[:, :, 2, :], op=ALU.add)
        P.I("vector", "tensor_tensor", [t_lb, t_lbv], [t_lbv], out=lbs, in0=lbs, in1=lbe[:, :, 3, :], op=ALU.add)
        P.I("vector", "reciprocal", [t_lbv], [t_lbv], out=lbs, in_=lbs)
        if l == 0:
            P.I("vector", "memset", [t_lbv], [], partial=[t_lbv], ap=lbv, constant=0.0)
        else:
            P.I("vector", "tensor_copy", [t_lb, t_lbv], [], partial=[t_lbv], out=lbv, in_=lbe[:, :, 1, :])
            for j in range(2, l + 1):
                P.I("vector", "tensor_tensor", [t_lb, t_lbv], [t_lbv], out=lbv, in0=lbv, in1=lbe[:, :, j, :], op=ALU.add)
            P.I("vector", "tensor_tensor", [t_lbv], [t_lbv], out=lbv, in0=lbv, in1=lbs, op=ALU.mult)
        P.I("vector", "tensor_scalar", [t_lbv], [t_lbv], out=oml, in0=lbv, scalar1=-1.0, scalar2=1.0, op0=ALU.mult, op1=ALU.add)

        UZ = A.bf16(NT * 512).rearrange("p (t c) -> p t c", c=512)
        t_UZ = [P.tok(f"UZ{t}") for t in range(NT)]
        NS = 3
        sf = [A.f32(512) for _ in range(NS)]
        sf_t = [P.tok(f"sf{i}") for i in range(NS)]
        sf2 = [A.f32(512) for _ in range(NS)]
        sf2_t = [P.tok(f"sf2{i}") for i in range(NS)]
        sb = [A.bf16(512) for _ in range(NS)]
        sb_t = [P.tok(f"sb{i}") for i in range(NS)]
        sv = [A.bf16(2 * 258) for _ in range(NS)]
        sv_t = [P.tok(f"sv{i}") for i in range(NS)]
        for i in range(NS):
            P.I("gpsimd", "memset", [], [sv_t[i]], ap=sv[i], constant=1.0)
        bst = [A.f32(16) for _ in range(NS)]
        bst_t = [P.tok(f"bst{i}") for i in range(NS)]
        vnb = [A.bf16(512) for _ in range(NS)]
        vnb_t = [P.tok(f"vnb{i}") for i in range(NS)]
        def dtoks(name, n):
            g = P.grp(name)
            return [P.tok(f"{name}{i}", g) for i in range(n)]
        t_YA = dtoks("YA", NT)
        t_ZB = dtoks("ZB", NT)
        t_ZC = dtoks("ZC", NT)
        t_HQ = dtoks("HQ", NT)
        t_HV = dtoks("HV", NT)
        t_HG = dtoks("HG", 2 * NT)
        t_HK = dtoks("HK", 2 * NT)
        t_QTs = dtoks("QTs", 8)
        t_KTloc = P.tok("KTloc", P.grp("KTloc"))
        t_KTc = P.tok("KTc", P.grp("KTc"))
        t_Vloc = P.tok("Vloc", P.grp("Vloc"))
        t_Vcx = P.tok("Vcx", P.grp("Vcx"))

        cnt = [0]

        def slot():
            cnt[0] += 1
            return cnt[0] % NS

        gmlp_pending = []
        wi = 0
        gorder = GROUP_ORDER
        if isinstance(debug, str) and debug.startswith("A:"):
            gorder = [int(v) for v in debug[2:].split(",")]
        def load_w(gi_):
            g_ = gorder[gi_]
            for q4 in range(4):
                P.dma("gpsimd", Wb[gi_ % 2][:, q4 * 4:(q4 + 1) * 4, :], win_in[l, q4 * 512:(q4 + 1) * 512, g_ * 512:(g_ + 1) * 512].rearrange("(k p) n -> p k n", p=128), [], [Wb_t[gi_ % 2]] if q4 == 0 else [], partial=[] if q4 == 0 else [Wb_t[gi_ % 2]])
        PREF = bool(os.environ.get("KPREF"))
        if PREF:
            load_w(0)
        for gi, g in enumerate(gorder):
            wb = Wb[gi % 2]
            wt = Wb_t[gi % 2]
            if not PREF:
                load_w(gi)
            elif gi + 1 < len(gorder):
                load_w(gi + 1)
            if g in (8, 9, 10, 11):
                isq = g in (8, 9)
                for j in range(4):
                    blk = (g - (8 if isq else 10)) * 4 + j
                    for ch, (t0, tn) in enumerate(((0, 512), (512, 512), (1024, 256))):
                        pb = (blk * 3 + ch) % 4
                        pp = ps[pb]
                        tiles = list(range(t0 // 128, (t0 + tn) // 128))
                        for kc in range(16):
                            P.I("tensor", "matmul", [wt] + [t_hT[t] for t in tiles], [], partial=[ps_t[pb]], out=pp[:, 0:tn], lhsT=wb[:, kc, j * 128:(j + 1) * 128], rhs=hT[:, kc, t0:t0 + tn], start=(kc == 0), stop=(kc == 15))
                        s = slot()
                        if ch < 2 and not os.environ.get('KDBG_NOROPE'):
                            P.I("scalar", "copy", [ps_t[pb]], [sb_t[s]], out=sb[s], in_=pp[:, 0:512])
                            pr = ps[4 + (ch % 2)]
                            prt = ps_t[4 + (ch % 2)]
                            kd = os.environ.get("KDBG_ROPE", "")
                            if "nomm" not in kd:
                                P.I("tensor", "matmul", [sb_t[s], t_const], [prt], out=pr[:, 0:512], lhsT=perm, rhs=sb[s], start=True, stop=True)
                            else:
                                pr, prt = pp, ps_t[pb]
                            if "nodve" not in kd:
                                P.I("vector", "tensor_tensor", [ps_t[pb], t_rope, sb_t[s]], [sf_t[s]], out=sf[s], in0=pp[:, 0:512], in1=rope[:, t0:t0 + 512], op=ALU.mult)
                                P.I("vector", "tensor_tensor", [prt, t_rope], [sf2_t[s]], out=sf2[s], in0=pr[:, 0:512], in1=rope[:, LAT + t0:LAT + t0 + 512], op=ALU.mult)
                            else:
                                P.I("scalar", "copy", [ps_t[pb]], [sf_t[s]], out=sf[s], in_=pp[:, 0:512])
                                P.I("scalar", "copy", [prt], [sf2_t[s]], out=sf2[s], in_=pr[:, 0:512])
                            s2 = slot()
                            P.I("vector" if "nogps" in kd else "gpsimd", "tensor_tensor", [sf_t[s], sf2_t[s]], [sb_t[s2]], out=sb[s2], in0=sf[s], in1=sf2[s], op=ALU.add)
                            if isq:
                                P.dma("sync", QTs[blk * 128:(blk + 1) * 128, t0:t0 + 512], sb[s2], [sb_t[s2]], [], partial=[t_QTs[blk]])
                            else:
                                P.dma("sync", KTloc[blk * 128:(blk + 1) * 128, t0:t0 + 512], sb[s2], [sb_t[s2]], [], partial=[t_KTloc])
                        else:
                            P.I("scalar", "copy", [ps_t[pb]], [sb_t[s]], out=sb[s][:, 0:256], in_=pp[:, 0:256])
                            if isq:
                                P.dma("sync", QTs[blk * 128:(blk + 1) * 128, 1024:1280], sb[s][:, 0:256], [sb_t[s]], [], partial=[t_QTs[blk]])
                            else:
                                P.dma("sync", KTc[blk * 128:(blk + 1) * 128, :], sb[s][:, 0:256], [sb_t[s]], [], partial=[t_KTc])
                continue
            order = list(range(NT))
            for t in order:
                pb = wi % 4
                wi += 1
                pp = ps[pb]
                for kc in range(16):
                    P.I("tensor", "matmul", [wt, t_hT[t]], [], partial=[ps_t[pb]], out=pp[:, :], lhsT=hT[:, kc, t * 128:(t + 1) * 128], rhs=wb[:, kc, :], start=(kc == 0), stop=(kc == 15))
                rows = slice(t * 128, (t + 1) * 128)
                if g == 2:
                    P.I("scalar", "activation", [ps_t[pb]], [t_UZ[t]], out=UZ[:, t, :], in_=pp[:, :], func=AF.Silu)
                elif g == 0:
                    s = slot()
                    P.I("scalar", "activation", [ps_t[pb]], [sf_t[s]], out=sf[s], in_=pp[:, :], func=AF.Gelu)
                    P.I("gpsimd", "tensor_tensor", [sf_t[s], t_UZ[t]], [t_UZ[t]], out=UZ[:, t, :], in0=sf[s], in1=UZ[:, t, :], op=ALU.mult)
                elif g == 1:
                    s = slot()
                    P.I("scalar", "activation", [ps_t[pb]], [sf_t[s]], out=sf[s], in_=pp[:, :], func=AF.Gelu)
                    bs6 = bst[s][:, 0:6]
                    mv = bst[s][:, 8:10]
                    P.I("vector", "bn_stats", [sf_t[s]], [bst_t[s]], out=bs6, in_=sf[s])
                    P.I("vector", "bn_aggr", [bst_t[s]], [bst_t[s]], out=mv, in_=bs6)
                    rstd = bst[s][:, 10:11]
                    P.I("scalar", "activation", [bst_t[s]], [bst_t[s]], out=rstd, in_=bst[s][:, 9:10], func=AF.Sqrt, bias=EPS)
                    P.I("vector", "reciprocal", [bst_t[s]], [bst_t[s]], out=rstd, in_=rstd)
                    P.I("vector", "tensor_scalar", [sf_t[s], bst_t[s]], [sf_t[s]], out=sf[s], in0=sf[s], scalar1=bst[s][:, 8:9], scalar2=rstd, op0=ALU.subtract, op1=ALU.mult)
                    P.I("vector", "tensor_tensor", [sf_t[s], t_ln], [sf_t[s]], out=sf[s], in0=sf[s], in1=lnw, op=ALU.mult)
                    P.I("gpsimd", "tensor_tensor", [sf_t[s], t_ln], [vnb_t[s]], out=vnb[s], in0=sf[s], in1=lnb, op=ALU.add)
                    gmlp_pending.append((t, s))
                    if len(gmlp_pending) > 1 or t == order[-1]:
                        while gmlp_pending and (len(gmlp_pending) > 1 or t == order[-1]):
                            tt, ss_ = gmlp_pending.pop(0)
                            pm = ps[4 + (tt % 2)]
                            pmt = ps_t[4 + (tt % 2)]
                            for h in range(4):
                                P.I("tensor", "matmul", [vnb_t[ss_], t_wsT], [], partial=[pmt], out=pm[:, h * 128:(h + 1) * 128], lhsT=wsT[:, h * 128:(h + 1) * 128], rhs=vnb[ss_][:, h * 128:(h + 1) * 128], start=True, stop=True)
                            s3 = slot()
                            for h in range(4):
                                P.I("vector", "scalar_tensor_tensor", [pmt, t_gbs, t_UZ[tt]], [], partial=[sb_t[s3]], out=sb[s3][:, h * 128:(h + 1) * 128], in0=pm[:, h * 128:(h + 1) * 128], scalar=gbs[:, l * 4 + h:l * 4 + h + 1], in1=UZ[:, tt, h * 128:(h + 1) * 128], op0=ALU.add, op1=ALU.mult)
                            P.dma("sync", YA[tt * 128:(tt + 1) * 128, :], sb[s3], [sb_t[s3]], [t_YA[tt]])
                elif g in (7, 14, 15, 3):
                    s = slot()
                    P.I("scalar", "activation", [ps_t[pb]], [sb_t[s]], out=sb[s], in_=pp[:, :], func=AF.Silu)
                    if g == 7:
                        P.dma("sync", ZB[rows, :], sb[s], [sb_t[s]], [t_ZB[t]])
                    elif g == 3:
                        P.dma("sync", HQ[rows, :], sb[s], [sb_t[s]], [t_HQ[t]])
                    else:
                        c0 = (g - 14) * 512
                        P.dma("sync", ZC[rows, c0:c0 + 512], sb[s], [sb_t[s]], [] if g == 15 else [t_ZC[t]], partial=[t_ZC[t]] if g == 15 else [])
                elif g == 4:
                    s = slot()
                    P.I("scalar", "copy", [ps_t[pb]], [sb_t[s]], out=sb[s], in_=pp[:, :])
                    P.dma("sync", HV[rows, :], sb[s], [sb_t[s]], [t_HV[t]])
                elif g in (12, 13):
                    s = slot()
                    P.I("vector", "tensor_copy", [ps_t[pb]], [], partial=[sv_t[s]], out=sv[s].rearrange("p (h c) -> p h c", c=258)[:, :, 0:256], in_=pp[:, :].rearrange("p (h c) -> p h c", c=256))
                    c0 = (g - 12) * 516
                    if t < 8:
                        P.dma("sync", Vloc[rows, c0:c0 + 516], sv[s], [sv_t[s]], [], partial=[t_Vloc])
                    else:
                        P.dma("sync", Vcx[(t - 8) * 128:(t - 7) * 128, c0:c0 + 516], sv[s], [sv_t[s]], [], partial=[t_Vcx])
                elif g in (5, 6):
                    d = g - 5
                    s = slot()
                    P.I("scalar", "activation", [ps_t[pb]], [sf_t[s]], out=sf[s], in_=pp[:, :], func=AF.Sigmoid)
                    P.I("vector", "tensor_tensor", [sf_t[s], t_lbv], [sf_t[s]], out=sf[s], in0=sf[s], in1=oml[:, d, :], op=ALU.mult)
                    P.I("vector", "tensor_tensor", [sf_t[s], t_lbv], [sf_t[s]], out=sf[s], in0=sf[s], in1=lbv[:, d, :], op=ALU.add)
                    P.I("scalar", "activation", [sf_t[s]], [sf2_t[s]], out=sf2[s], in_=sf[s], func=AF.Ln)
                    P.I("gpsimd", "tensor_scalar", [sf_t[s]], [sb_t[s]], out=sb[s], in0=sf[s], scalar1=-1.0, scalar2=1.0, op0=ALU.mult, op1=ALU.add)
                    P.dma("sync", HG[d, rows, :], sf2[s], [sf2_t[s]], [t_HG[d * NT + t]])
                    P.dma("sync", HK[d, rows, :], sb[s], [sb_t[s]], [t_HK[d * NT + t]])
            if g == 13:
                t_KTgat = P.tok("KTgat")
                t_Vgat = P.tok("Vgat")
                P.cc(lambda: nc.gpsimd.collective_compute("AllGather", ALU.bypass, replica_groups=[list(range(NCORES))], ins=[KTloc.opt()], outs=[KTgat.opt()]), [t_KTloc], [t_KTgat])
                P.cc(lambda: nc.gpsimd.collective_compute("AllGather", ALU.bypass, replica_groups=[list(range(NCORES))], ins=[Vloc.opt()], outs=[Vgat.opt()]), [t_Vloc], [t_Vgat])
        P.barrier()
        if isinstance(debug, str) and debug.startswith("A"):
            break
        A.reset()
        t_OFB = dtoks("OFB", 2 * NT)
        t_QHAT = dtoks("QHAT", 16)
        t_HSloc = P.tok("HSloc", P.grp("HSloc"))
        t_SCTX = P.tok("SCTX", P.grp("SCTX"))
        R2 = 2
        gq = [A.bf16(512) for _ in range(R2)]; gq_t = [P.tok(f"gq{i}") for i in range(R2)]
        gv = [A.bf16(512) for _ in range(R2)]; gv_t = [P.tok(f"gv{i}") for i in range(R2)]
        gg = [A.f32(512) for _ in range(R2)]; gg_t = [P.tok(f"gg{i}") for i in range(R2)]
        gk = [A.bf16(512) for _ in range(R2)]; gk_t = [P.tok(f"gk{i}") for i in range(R2)]
        ghi = [A.bf16(512) for _ in range(R2)]; ghi_t = [P.tok(f"ghi{i}") for i in range(R2)]
        glo = [A.bf16(512) for _ in range(R2)]; glo_t = [P.tok(f"glo{i}") for i in range(R2)]
        Eq = [A.f32(512) for _ in range(R2)]; Eq_t = [P.tok(f"Eq{i}") for i in range(R2)]
        Ek = [A.f32(512) for _ in range(R2)]; Ek_t = [P.tok(f"Ek{i}") for i in range(R2)]
        Ed = [A.f32(512) for _ in range(R2)]; Ed_t = [P.tok(f"Ed{i}") for i in range(R2)]
        qt_ = [A.bf16(512) for _ in range(R2)]; qt_t = [P.tok(f"qt{i}") for i in range(R2)]
        kt_ = [A.bf16(512) for _ in range(R2)]; kt_t = [P.tok(f"kt{i}") for i in range(R2)]
        kd_ = [A.bf16(512) for _ in range(R2)]; kd_t = [P.tok(f"kd{i}") for i in range(R2)]
        qT = [A.bf16(512) for _ in range(R2)]; qT_t = [P.tok(f"qT{i}") for i in range(R2)]
        kT = [A.bf16(512) for _ in range(R2)]; kT_t = [P.tok(f"kT{i}") for i in range(R2)]
        qTz = [A.bf16(2048) for _ in range(R2)]; qTz_t = [P.tok(f"qTz{i}") for i in range(R2)]
        PTb = [A.bf16(512) for _ in range(R2)]; PT_t = [P.tok(f"PT{i}") for i in range(R2)]
        craw = [A.f32(32) for _ in range(R2)]; craw_t = [P.tok(f"craw{i}") for i in range(R2)]
        cexp = [A.f32(32) for _ in range(R2)]; cexp_t = [P.tok(f"cexp{i}") for i in range(R2)]
        arg = [A.f32(16) for _ in range(R2)]; arg_t = [P.tok(f"arg{i}") for i in range(R2)]
        qs = [A.f32(16) for _ in range(R2)]
        vm = [A.bf16(512) for _ in range(4)]; vm_t = [P.tok(f"vm{i}") for i in range(4)]; qs_t = [P.tok(f"qs{i}") for i in range(R2)]
        ost = [A.f32(512) for _ in range(R2)]; ost_t = [P.tok(f"ost{i}") for i in range(R2)]
        qh = [A.bf16(512) for _ in range(R2)]; qh_t = [P.tok(f"qh{i}") for i in range(R2)]
        Sst = A.f32(512); t_S = P.tok("S")
        Sbf = [A.bf16(512) for _ in range(2)]; Sbf_t = [P.tok(f"Sbf{i}") for i in range(2)]
        Bacc = A.f32(4); Btmp = A.f32(4); t_B = P.tok("Bacc")
        Dst = A.f32(4); t_Dst = P.tok("Dst")
        for i in range(R2):
            P.I("gpsimd", "memset", [], [qTz_t[i]], ap=qTz[i], constant=0.0)
        psA, psB, psC, psS, psO, psU = ps
        tA, tB, tC, tS, tO, tU = ps_t
        it = 0
        for d in range(2):
            scan = ([8, 9] + list(range(8))) if d == 0 else ([9, 8] + list(range(7, -1, -1)))
            P.I("vector", "memset", [], [t_S], ap=Sst, constant=0.0)
            P.I("vector", "memset", [], [t_B], ap=Bacc, constant=0.0)
            for si, t in enumerate(scan):
                b = it % R2
                it += 1
                lat = t < 8
                rows = slice(t * 128, (t + 1) * 128)
                if si == 2:
                    P.dma("sync", SCTX[d * 128:(d + 1) * 128, :], Sst, [t_S], [], partial=[t_SCTX])
                    P.I("vector", "memset", [t_S], [t_S], ap=Sst, constant=0.0)
                P.dma("sync", gq[b], HQ[rows, :], [t_HQ[t]], [gq_t[b]])
                P.dma("sync", gv[b], HV[rows, :], [t_HV[t]], [gv_t[b]])
                P.dma("sync", gg[b], HG[d, rows, :], [t_HG[d * NT + t]], [gg_t[b]])
                P.dma("sync", gk[b], HK[d, rows, :], [t_HK[d * NT + t]], [gk_t[b]])
                P.I("scalar", "copy", [gg_t[b]], [ghi_t[b]], out=ghi[b], in_=gg[b])
                P.I("gpsimd", "tensor_tensor", [gg_t[b], ghi_t[b]], [glo_t[b]], out=glo[b], in0=gg[b], in1=ghi[b], op=ALU.subtract)
                for (pp_, tp_, M_) in ((psA, tA, Mq[d]), (psB, tB, Mdec[d])):
                    P.I("tensor", "matmul", [ghi_t[b], t_const], [tp_], out=pp_[:, :], lhsT=M_, rhs=ghi[b], start=True, stop=False)
                    P.I("tensor", "matmul", [glo_t[b], t_const], [], partial=[tp_], out=pp_[:, :], lhsT=M_, rhs=glo[b], start=False, stop=True)
                for h in range(4):
                    hs = slice(h * 128, (h + 1) * 128)
                    P.I("tensor", "matmul", [ghi_t[b], t_const], [tC] if h == 0 else [], partial=[] if h == 0 else [tC], out=psC[:, h * 8:(h + 1) * 8], lhsT=ghi[b][:, hs], rhs=selv[d], start=True, stop=False)
                    P.I("tensor", "matmul", [glo_t[b], t_const], [], partial=[tC], out=psC[:, h * 8:(h + 1) * 8], lhsT=glo[b][:, hs], rhs=selv[d], start=False, stop=True)
                P.I("scalar", "activation", [tA], [Eq_t[b]], out=Eq[b], in_=psA[:, :], func=AF.Exp)
                P.I("scalar", "activation", [tA], [Ek_t[b]], out=Ek[b], in_=psA[:, :], func=AF.Exp, scale=-1.0)
                P.I("scalar", "activation", [tB], [Ed_t[b]], out=Ed[b], in_=psB[:, :], func=AF.Exp)
                P.I("scalar", "copy", [tC], [craw_t[b]], out=craw[b], in_=psC[:, 0:32])
                P.I("scalar", "activation", [craw_t[b]], [cexp_t[b]], out=cexp[b], in_=craw[b], func=AF.Exp)
                cr = craw[b].rearrange("p (h c) -> p h c", c=8)
                ce = cexp[b].rearrange("p (h c) -> p h c", c=8)
                corder = (0, 1, 2, 3) if d == 0 else (3, 2, 1, 0)
                if lat:
                    av = arg[b].rearrange("p (h c) -> p h c", c=4)
                    for ci_, c in enumerate(corder):
                        P.I("vector", "tensor_tensor", [craw_t[b], t_B], [arg_t[b]] if ci_ == 0 else [], partial=[] if ci_ == 0 else [arg_t[b]], out=av[:, :, c], in0=cr[:, :, c], in1=Bacc, op=ALU.add)
                        P.I("vector", "tensor_tensor", [craw_t[b], t_B], [t_B], out=Bacc, in0=cr[:, :, 4 + c], in1=Bacc, op=ALU.add)
                    P.I("scalar", "activation", [arg_t[b]], [qs_t[b]], out=qs[b], in_=arg[b], func=AF.Exp)
                for c in range(4):
                    P.I("gpsimd", "tensor_scalar", [gv_t[b], t_rmask], [vm_t[c]], out=vm[c], in0=gv[b], scalar1=rmask[:, c:c + 1], scalar2=None, op0=ALU.mult)
                P.I("gpsimd", "tensor_tensor", [gq_t[b], Eq_t[b]], [qt_t[b]], out=qt_[b], in0=gq[b], in1=Eq[b], op=ALU.mult)
                P.I("gpsimd", "tensor_tensor", [gk_t[b], Ek_t[b]], [kt_t[b]], out=kt_[b], in0=gk[b], in1=Ek[b], op=ALU.mult)
                P.I("gpsimd", "tensor_tensor", [gk_t[b], Ed_t[b]], [kd_t[b]], out=kd_[b], in0=gk[b], in1=Ed[b], op=ALU.mult)
                for h in range(4):
                    hs = slice(h * 128, (h + 1) * 128)
                    P.I("tensor", "transpose", [qt_t[b], t_const], [psT_t[0]] if h == 0 else [], partial=[] if h == 0 else [psT_t[0]], out=psT[0][:, hs], in_=qt_[b][:, hs], identity=ident)
                for h in range(4):
                    hs = slice(h * 128, (h + 1) * 128)
                    P.I("tensor", "transpose", [kt_t[b], t_const], [psT_t[1]] if h == 0 else [], partial=[] if h == 0 else [psT_t[1]], out=psT[1][:, hs], in_=kt_[b][:, hs], identity=ident)
                P.I("scalar", "copy", [psT_t[0]], [qT_t[b]], out=qT[b], in_=psT[0][:, 0:512])
                P.I("vector", "tensor_copy", [psT_t[1]], [kT_t[b]], out=kT[b], in_=psT[1][:, 0:512])
                qz = qTz[b].rearrange("p (h x) -> p h x", x=512)
                q3 = qT[b].rearrange("p (h x) -> p h x", x=128)
                for c in range(4):
                    P.I("gpsimd", "tensor_copy", [qT_t[b]], [], partial=[qTz_t[b]], out=qz[:, :, c * 160:c * 160 + 32], in_=q3[:, :, c * 32:(c + 1) * 32])
                if lat:
                    for h in range(4):
                        for c in range(4):
                            P.I("vector", "tensor_scalar", [qT_t[b], qs_t[b]], [], partial=[qh_t[b]], out=qh[b][:, h * 128 + c * 32:h * 128 + (c + 1) * 32], in0=qT[b][:, h * 128 + c * 32:h * 128 + (c + 1) * 32], scalar1=qs[b][:, h * 4 + c:h * 4 + c + 1], scalar2=None, op0=ALU.mult)
                    P.dma("sync", QHAT[(d * 8 + t) * 128:(d * 8 + t + 1) * 128, :], qh[b], [qh_t[b]], [t_QHAT[d * 8 + t]])
                for h in range(4):
                    hs = slice(h * 128, (h + 1) * 128)
                    P.I("tensor", "matmul", [kT_t[b], qT_t[b]], [tS] if h == 0 else [], partial=[] if h == 0 else [tS], out=psS[:, hs], lhsT=kT[b][:, hs], rhs=qT[b][:, hs], start=True, stop=True)
                for h in range(4):
                    hs = slice(h * 128, (h + 1) * 128)
                    P.I("vector", "tensor_tensor", [tS, t_const], [PT_t[b]] if h == 0 else [], partial=[] if h == 0 else [PT_t[b]], out=PTb[b][:, hs], in0=psS[:, hs], in1=Mmask[d], op=ALU.mult)
                for h in range(4):
                    hs = slice(h * 128, (h + 1) * 128)
                    P.I("tensor", "matmul", [PT_t[b], gv_t[b]], [tO] if h == 0 else [], partial=[] if h == 0 else [tO], out=psO[:, hs], lhsT=PTb[b][:, hs], rhs=gv[b][:, hs], start=(h == 0), stop=False)
                for ci_, c in enumerate(corder):
                    sbi = ci_ % 2
                    for h in range(4):
                        hs = slice(h * 128, (h + 1) * 128)
                        P.I("vector", "tensor_scalar", [t_S, cexp_t[b]], [Sbf_t[sbi]] if h == 0 else [], partial=[] if h == 0 else [Sbf_t[sbi]], out=Sbf[sbi][:, hs], in0=Sst[:, hs], scalar1=ce[:, h, c:c + 1], scalar2=None, op0=ALU.mult)
                    for h in range(4):
                        hs = slice(h * 128, (h + 1) * 128)
                        P.I("tensor", "matmul", [qTz_t[b], Sbf_t[sbi]], [], partial=[tO], out=psO[:, hs], lhsT=qz[:, h, c * 128:(c + 1) * 128], rhs=Sbf[sbi][:, hs], start=False, stop=(ci_ == 3 and h == 3))
                    for h in range(4):
                        hs = slice(h * 128, (h + 1) * 128)
                        P.I("tensor", "matmul", [kd_t[b], vm_t[c]], [tU] if h == 0 else [], partial=[] if h == 0 else [tU], out=psU[:, hs], lhsT=kd_[b][:, hs], rhs=vm[c][:, hs], start=True, stop=True)
                    for h in range(4):
                        hs = slice(h * 128, (h + 1) * 128)
                        P.I("vector", "scalar_tensor_tensor", [t_S, cexp_t[b], tU], [], partial=[t_S], out=Sst[:, hs], in0=Sst[:, hs], scalar=ce[:, h, 4 + c:5 + c], in1=psU[:, hs], op0=ALU.mult, op1=ALU.add)
                P.I("scalar", "copy", [tO], [ost_t[b]], out=ost[b], in_=psO[:, :])
                P.dma("sync", OFB[d, rows, :], ost[b], [ost_t[b]], [t_OFB[d * NT + t]])
            P.dma("sync", HSloc[d * 128:(d + 1) * 128, 0:512], Sst, [t_S], [], partial=[t_HSloc])
            P.I("scalar", "activation", [t_B], [t_Dst], out=Dst, in_=Bacc, func=AF.Exp)
            P.dma("sync", HSloc[d * 128:(d + 1) * 128, 512:516], Dst, [t_Dst], [], partial=[t_HSloc])
        t_HSgat = P.tok("HSgat")
        P.cc(lambda: nc.gpsimd.collective_compute("AllGather", ALU.bypass, replica_groups=[list(range(NCORES))], ins=[HSloc.opt()], outs=[HSgat.opt()]), [t_HSloc], [t_HSgat])
        P.barrier()
        if debug == "B":
            break
        A.reset()
        NKT = 66
        KTh = A.bf16(2 * 8448).rearrange("p (m t) -> p m t", m=2)
        Vh = A.bf16(NKT * 258).rearrange("p (j c) -> p j c", c=258)
        kv_t = [P.tok(f"kv{i}") for i in range(9)]
        Qg = [A.bf16(512).rearrange("p (m q) -> p m q", m=2) for _ in range(2)]
        Qg_t = [P.tok(f"Qg{i}") for i in range(2)]
        Eb = [A.bf16(512) for _ in range(3)]
        Eb_t = [P.tok(f"Eb{i}") for i in range(3)]
        lp = A.f32(512)
        t_lp = P.tok("lp")
        lamw = A.f32(256 + 8)
        t_lam = P.tok("lam")
        subw = A.f32(256)
        t_subw = P.tok("subw")
        zc = [A.bf16(256) for _ in range(2)]
        zc_t = [P.tok(f"zc{i}") for i in range(2)]
        od = [A.f32(256) for _ in range(2)]
        od_t = [P.tok(f"od{i}") for i in range(2)]
        tmp1 = [A.f32(256) for _ in range(2)]
        tmp1_t = [P.tok(f"tmp1{i}") for i in range(2)]
        ycb = [A.bf16(256) for _ in range(2)]
        ycb_t = [P.tok(f"ycb{i}") for i in range(2)]
        sq = A.bf16(256)
        t_sq = P.tok("sq")
        rr = [A.f32(8) for _ in range(2)]
        rr_t = [P.tok(f"rr{i}") for i in range(2)]
        t_YC = dtoks("YC", NT)
        P.dma("sync", lp, dlam_in[l, :].partition_broadcast(128), [], [t_lp])
        lpv = lp.rearrange("p (a b c) -> p a b c", a=2, b=2)
        P.I("vector", "tensor_tensor", [t_lp], [t_lam], out=lamw[:, 0:256].rearrange("p (a c) -> p a c", a=2), in0=lpv[:, :, 0, :], in1=lpv[:, :, 1, :], op=ALU.mult)
        P.I("vector", "tensor_reduce", [t_lam], [t_lam], out=lamw[:, 256:258], in_=lamw[:, 0:256].rearrange("p (a c) -> p a c", a=2), axis=mybir.AxisListType.X, op=ALU.add)
        P.I("scalar", "activation", [t_lam], [t_lam], out=lamw[:, 258:260], in_=lamw[:, 256:258], func=AF.Exp)
        P.I("vector", "tensor_tensor", [t_lam], [t_lam], out=lamw[:, 260:261], in0=lamw[:, 258:259], in1=lamw[:, 259:260], op=ALU.subtract)
        P.I("vector", "tensor_scalar", [t_lam], [t_lam], out=lamw[:, 260:261], in0=lamw[:, 260:261], scalar1=float(lam_init), scalar2=None, op0=ALU.add)
        lamcol = lamw[:, 260:261]
        P.dma("sync", subw, dsub_in[l, :].partition_broadcast(128), [], [t_subw])
        P.I("vector", "tensor_scalar", [t_subw], [t_subw], out=subw, in0=subw, scalar1=float(1.0 - lam_init), scalar2=None, op0=ALU.mult)
        acc = [[ps[2], ps[3]], [ps[4], ps[5]]]
        acc_t = [[ps_t[2], ps_t[3]], [ps_t[4], ps_t[5]]]
        SCALE = float(128 ** -0.5)
        uq = 0
        ue = 0
        uo = 0
        for h in range(4):
            for r_ in range(8):
                P.dma("sync", KTh[:, :, r_ * 1024:(r_ + 1) * 1024], KTgat[r_ * 1024 + h * 256:r_ * 1024 + (h + 1) * 256, :].rearrange("(m d) t -> d m t", d=128), [t_KTgat], [kv_t[r_]])
                P.dma("sync", Vh[:, r_ * 8:(r_ + 1) * 8, :], Vgat[r_ * 1024:(r_ + 1) * 1024, h * 258:(h + 1) * 258].rearrange("(j p) c -> p j c", p=128), [t_Vgat], [], partial=[kv_t[r_]])
            P.dma("sync", KTh[:, :, 8192:8448], KTc[h * 256:(h + 1) * 256, :].rearrange("(m d) t -> d m t", d=128), [t_KTc], [kv_t[8]])
            P.dma("sync", Vh[:, 64:66, :], Vcx[:, h * 258:(h + 1) * 258].rearrange("(j p) c -> p j c", p=128), [t_Vcx], [], partial=[kv_t[8]])
            def load_q(qg_):
                P.dma("sync", Qg[qg_ % 2], QTs[h * 256:(h + 1) * 256, qg_ * 256:(qg_ + 1) * 256].rearrange("(m d) q -> d m q", d=128), [t_QTs[2 * h], t_QTs[2 * h + 1]], [Qg_t[qg_ % 2]])
            load_q(0)
            for qg in range(5):
                q0 = qg * 256
                kts = list(range(NKT)) if qg < 4 else [64, 65]
                qb = qg % 2

                def qk(ki_):
                    kt_ = kts[ki_]
                    sbk_ = ki_ % 2
                    kvt_ = kv_t[min(kt_ // 8, 8)]
                    for m in range(2):
                        P.I("tensor", "matmul", [kvt_, Qg_t[qb]], [ps_t[sbk_]] if m == 0 else [], partial=[] if m == 0 else [ps_t[sbk_]], out=ps[sbk_][:, m * 256:(m + 1) * 256], lhsT=KTh[:, m, kt_ * 128:(kt_ + 1) * 128], rhs=Qg[qb][:, m, :], start=True, stop=True)
                qk(0)
                for ki, kt in enumerate(kts):
                    sbk = ki % 2
                    pS = ps[sbk]
                    kvt = kv_t[min(kt // 8, 8)]
                    if ki + 1 < len(kts):
                        qk(ki + 1)
                    eb = ue % 3
                    ue += 1
                    P.I("scalar", "activation", [ps_t[sbk]], [Eb_t[eb]], out=Eb[eb], in_=pS[:, :], func=AF.Exp, scale=SCALE)
                    for m in range(2):
                        for qi in range(2):
                            P.I("tensor", "matmul", [Eb_t[eb], kvt], [acc_t[m][qi]] if ki == 0 else [], partial=[] if ki == 0 else [acc_t[m][qi]], out=acc[m][qi][:, 0:258], lhsT=Eb[eb][:, m * 256 + qi * 128:m * 256 + (qi + 1) * 128], rhs=Vh[:, kt, :], start=(ki == 0), stop=(ki == len(kts) - 1))
                if qg + 1 < 5:
                    load_q(qg + 1)
                for qi in range(2):
                    t = (q0 + qi * 128) // 128
                    rows = slice(t * 128, (t + 1) * 128)
                    ob = uo % 2
                    uo += 1
                    P.dma("sync", zc[ob], ZC[rows, h * 256:(h + 1) * 256], [t_ZC[t]], [zc_t[ob]])
                    o0, o1 = acc[0][qi], acc[1][qi]
                    r = rr[ob]
                    P.I("vector", "reciprocal", [acc_t[0][qi]], [rr_t[ob]], out=r[:, 0:1], in_=o0[:, 256:257])
                    P.I("vector", "reciprocal", [acc_t[1][qi], rr_t[ob]], [rr_t[ob]], out=r[:, 1:2], in_=o1[:, 256:257])
                    P.I("vector", "tensor_tensor", [rr_t[ob], t_lam], [rr_t[ob]], out=r[:, 1:2], in0=r[:, 1:2], in1=lamcol, op=ALU.mult)
                    P.I("vector", "tensor_scalar", [acc_t[1][qi], rr_t[ob]], [tmp1_t[ob]], out=tmp1[ob], in0=o1[:, 0:256], scalar1=r[:, 1:2], scalar2=None, op0=ALU.mult)
                    P.I("vector", "scalar_tensor_tensor", [acc_t[0][qi], rr_t[ob], tmp1_t[ob]], [od_t[ob]], out=od[ob], in0=o0[:, 0:256], scalar=r[:, 0:1], in1=tmp1[ob], op0=ALU.mult, op1=ALU.subtract)
                    P.I("scalar", "activation", [od_t[ob]], [t_sq, rr_t[ob]], out=sq, in_=od[ob], func=AF.Square, accum_out=r[:, 2:3])
                    P.I("scalar", "activation", [rr_t[ob]], [rr_t[ob]], out=r[:, 3:4], in_=r[:, 2:3], func=AF.Sqrt, scale=1.0 / 256, bias=EPS)
                    P.I("vector", "reciprocal", [rr_t[ob]], [rr_t[ob]], out=r[:, 3:4], in_=r[:, 3:4])
                    P.I("vector", "scalar_tensor_tensor", [od_t[ob], rr_t[ob], t_subw], [od_t[ob]], out=od[ob], in0=od[ob], scalar=r[:, 3:4], in1=subw, op0=ALU.mult, op1=ALU.mult)
                    P.I("gpsimd", "tensor_tensor", [od_t[ob], zc_t[ob]], [ycb_t[ob]], out=ycb[ob], in0=od[ob], in1=zc[ob], op=ALU.mult)
                    P.dma("sync", YC[rows, h * 256:(h + 1) * 256], ycb[ob], [ycb_t[ob]], [t_YC[t]] if h == 0 else [], partial=[] if h == 0 else [t_YC[t]])
        P.barrier()
        if debug == "C":
            break

        A.reset()
        HSall = A.f32(8 * 2 * 516).rearrange("p (r d c) -> p r d c", r=8, d=2)
        t_HSall = P.tok("HSall")
        for r_ in range(8):
            P.dma("sync", HSall[:, r_, :, :], HSgat[r_ * 256:(r_ + 1) * 256, :].rearrange("(d k) c -> k d c", k=128), [t_HSgat], [t_HSall] if r_ == 0 else [], partial=[] if r_ == 0 else [t_HSall])
        Pst = A.f32(1024).rearrange("p (d c) -> p d c", d=2)
        t_Pst = P.tok("Pst")
        P.dma("sync", Pst, SCTX.rearrange("(d k) c -> k d c", k=128), [t_SCTX], [t_Pst])
        Sin = A.f32(1024).rearrange("p (d c) -> p d c", d=2)
        t_Sin = P.tok("Sin")
        Sinb = A.bf16(1024).rearrange("p (d c) -> p d c", d=2)
        t_Sinb = P.tok("Sinb")
        for d in range(2):
            order = list(range(8)) if d == 0 else list(range(7, -1, -1))
            for n_, j in enumerate(order):
                if n_ == 0:
                    P.I("vector", "tensor_scalar", [t_Pst, t_oneh], [t_Sin] if d == 0 else [], partial=[] if d == 0 else [t_Sin], out=Sin[:, d, :], in0=Pst[:, d, :], scalar1=oneh[:, j:j + 1], scalar2=None, op0=ALU.mult)
                else:
                    P.I("vector", "scalar_tensor_tensor", [t_Pst, t_oneh, t_Sin], [], partial=[t_Sin], out=Sin[:, d, :], in0=Pst[:, d, :], scalar=oneh[:, j:j + 1], in1=Sin[:, d, :], op0=ALU.mult, op1=ALU.add)
                if n_ < 7:
                    for h in range(4):
                        hs = slice(h * 128, (h + 1) * 128)
                        P.I("vector", "scalar_tensor_tensor", [t_Pst, t_HSall], [], partial=[t_Pst], out=Pst[:, d, hs], in0=Pst[:, d, hs], scalar=HSall[:, j, d, 512 + h:513 + h], in1=HSall[:, j, d, hs], op0=ALU.mult, op1=ALU.add)
        P.I("vector", "tensor_copy", [t_Sin], [t_Sinb], out=Sinb, in_=Sin)
        hnw = A.f32(128)
        t_hnw = P.tok("hnw")
        P.dma("sync", hnw, hnw_in[l, :].partition_broadcast(128), [], [t_hnw])
        t_YB = dtoks("YB", NT)
        of_ = [A.f32(512) for _ in range(2)]; of_t = [P.tok(f"of{i}") for i in range(2)]
        ob_ = [A.f32(512) for _ in range(2)]; ob_t = [P.tok(f"ob{i}") for i in range(2)]
        qhd = [[A.bf16(512) for _ in range(2)] for _ in range(2)]; qhd_t = [[P.tok(f"qhd{i}{j}") for j in range(2)] for i in range(2)]
        zb = [A.bf16(512) for _ in range(2)]; zb_t = [P.tok(f"zb{i}") for i in range(2)]
        ybb = [A.bf16(512) for _ in range(2)]; ybb_t = [P.tok(f"ybb{i}") for i in range(2)]
        sq2 = A.bf16(128); t_sq2 = P.tok("sq2")
        r4 = [A.f32(8) for _ in range(2)]; r4_t = [P.tok(f"r4{i}") for i in range(2)]
        for t in range(NT):
            b = t % 2
            rows = slice(t * 128, (t + 1) * 128)
            P.dma("sync", of_[b], OFB[0, rows, :], [t_OFB[t]], [of_t[b]])
            P.dma("sync", ob_[b], OFB[1, rows, :], [t_OFB[NT + t]], [ob_t[b]])
            P.dma("sync", zb[b], ZB[rows, :], [t_ZB[t]], [zb_t[b]])
            P.I("gpsimd", "tensor_tensor", [of_t[b], ob_t[b]], [of_t[b]], out=of_[b], in0=of_[b], in1=ob_[b], op=ALU.add)
            if t < 8:
                pX = ps[t % 2]
                pXt = ps_t[t % 2]
                for d in range(2):
                    P.dma("sync", qhd[b][d], QHAT[(d * 8 + t) * 128:(d * 8 + t + 1) * 128, :], [t_QHAT[d * 8 + t]], [qhd_t[b][d]])
                n_ = 0
                for h in range(4):
                    hs = slice(h * 128, (h + 1) * 128)
                    for d in range(2):
                        P.I("tensor", "matmul", [qhd_t[b][d], t_Sinb], [pXt] if n_ == 0 else [], partial=[] if n_ == 0 else [pXt], out=pX[:, hs], lhsT=qhd[b][d][:, hs], rhs=Sinb[:, d, hs], start=(n_ == 0), stop=(n_ == 7))
                        n_ += 1
                P.I("vector", "tensor_tensor", [pXt, of_t[b]], [of_t[b]], out=of_[b], in0=pX[:, :], in1=of_[b], op=ALU.add)
            for h in range(4):
                hs = slice(h * 128, (h + 1) * 128)
                P.I("scalar", "activation", [of_t[b]], [t_sq2, r4_t[b]] if h == 0 else [t_sq2], partial=[] if h == 0 else [r4_t[b]], out=sq2, in_=of_[b][:, hs], func=AF.Square, accum_out=r4[b][:, h:h + 1])
            P.I("scalar", "activation", [r4_t[b]], [r4_t[b]], out=r4[b][:, 4:8], in_=r4[b][:, 0:4], func=AF.Sqrt, scale=1.0 / 128, bias=EPS)
            P.I("vector", "reciprocal", [r4_t[b]], [r4_t[b]], out=r4[b][:, 4:8], in_=r4[b][:, 4:8])
            for h in range(4):
                hs = slice(h * 128, (h + 1) * 128)
                P.I("vector", "scalar_tensor_tensor", [of_t[b], r4_t[b], t_hnw], [], partial=[of_t[b]], out=of_[b][:, hs], in0=of_[b][:, hs], scalar=r4[b][:, 4 + h:5 + h], in1=hnw, op0=ALU.mult, op1=ALU.mult)
            P.I("gpsimd", "tensor_tensor", [of_t[b], zb_t[b]], [ybb_t[b]], out=ybb[b], in0=of_[b], in1=zb[b], op=ALU.mult)
            P.dma("sync", YB[rows, :], ybb[b], [ybb_t[b]], [t_YB[t]])
        P.barrier()
        if debug == "D":
            break

        A.reset()
        last = (l == DEPTH - 1)
        tiles_e = list(range(8)) if last else list(range(NT))
        yT = A.bf16(16 * TOK).rearrange("p (k t) -> p k t", t=TOK)
        t_yT = [P.tok(f"yT{t}") for t in range(NT)]
        Wo = [A.bf16(16 * 512).rearrange("p (k n) -> p k n", n=512) for _ in range(2)]
        Wo_t = [P.tok(f"Wo{i}") for i in range(2)]
        ytile = [A.bf16(D) for _ in range(2)]
        ytile_t = [P.tok(f"ytile{i}") for i in range(2)]
        Gt = [A.f32(D) for _ in range(2)]
        Gt_t = [P.tok(f"Gt{i}") for i in range(2)]
        xin = [A.f32(512) for _ in range(3)]
        xin_t = [P.tok(f"xin{i}") for i in range(3)]
        xo = [A.f32(512) for _ in range(3)]
        xo_t = [P.tok(f"xo{i}") for i in range(3)]
        g_XS = P.grp("XSw")
        t_XS = [P.tok(f"XS{t}", g_XS) for t in range(NT)]
        for s_ in range(2):
            P.dma("sync", Gt[s_].rearrange("p (r c) -> p r c", c=256), modvec(l, s_, 2).partition_broadcast(128), [t_modgat], [Gt_t[s_]])
        for t in tiles_e:
            b = t % 2
            rows = slice(t * 128, (t + 1) * 128)
            P.dma("sync", ytile[b][:, 0:512], YA[rows, :], [t_YA[t]], [ytile_t[b]])
            P.dma("sync", ytile[b][:, 512:1024], YB[rows, :], [t_YB[t]], [], partial=[ytile_t[b]])
            P.dma("sync", ytile[b][:, 1024:2048], YC[rows, :], [t_YC[t]], [], partial=[ytile_t[b]])
            for half in range(2):
                pt = psT[half]
                for j in range(8):
                    kc = half * 8 + j
                    P.I("tensor", "transpose", [ytile_t[b], t_const], [psT_t[half]] if j == 0 else [], partial=[] if j == 0 else [psT_t[half]], out=pt[:, j * 128:(j + 1) * 128], in_=ytile[b][:, kc * 128:(kc + 1) * 128], identity=ident)
                dsth = yT[:, half * 8:(half + 1) * 8, t * 128:(t + 1) * 128]
                srch = pt[:, :].rearrange("p (k t) -> p k t", t=128)
                if half == 0:
                    P.I("vector", "tensor_copy", [psT_t[half]], [t_yT[t]], out=dsth, in_=srch)
                else:
                    P.I("scalar", "copy", [psT_t[half]], [], partial=[t_yT[t]], out=dsth, in_=srch)
        ux = 0
        wq = 0
        def load_wo(gi_):
            for q4 in range(4):
                P.dma("gpsimd", Wo[gi_ % 2][:, q4 * 4:(q4 + 1) * 4, :], wout_in[l, q4 * 512:(q4 + 1) * 512, gi_ * 512:(gi_ + 1) * 512].rearrange("(k p) n -> p k n", p=128), [], [Wo_t[gi_ % 2]] if q4 == 0 else [], partial=[] if q4 == 0 else [Wo_t[gi_ % 2]])
        if PREF:
            load_wo(0)
        for gi in range(4):
            wb = Wo[gi % 2]
            wt = Wo_t[gi % 2]
            if not PREF:
                load_wo(gi)
            elif gi + 1 < 4:
                load_wo(gi + 1)
            for t in tiles_e:
                pb = wq % 4
                wq += 1
                pp = ps[pb]
                rows = slice(t * 128, (t + 1) * 128)
                cols = slice(gi * 512, (gi + 1) * 512)
                for kc in range(16):
                    P.I("tensor", "matmul", [wt, t_yT[t]], [ps_t[pb]] if kc == 0 else [], partial=[] if kc == 0 else [ps_t[pb]], out=pp[:, :], lhsT=yT[:, kc, t * 128:(t + 1) * 128], rhs=wb[:, kc, :], start=(kc == 0), stop=(kc == 15))
                xb = ux % 3
                ux += 1
                P.dma("sync", xin[xb], xrows(t)[:, cols], [], [xin_t[xb]])
                sg = 0 if t < 8 else 1
                P.I("vector", "tensor_tensor", [ps_t[pb], Gt_t[sg]], [xo_t[xb]], out=xo[xb], in0=pp[:, :], in1=Gt[sg][:, cols], op=ALU.mult)
                P.I("gpsimd", "tensor_tensor", [xo_t[xb], xin_t[xb]], [xo_t[xb]], out=xo[xb], in0=xo[xb], in1=xin[xb], op=ALU.add)
                P.dma("sync", xs_dst[rows, cols], xo[xb], [xo_t[xb]], [t_XS[t]] if gi == 0 else [], partial=[] if gi == 0 else [t_XS[t]])
        P.barrier()
        if not (last or l == n_layers - 1):
            P.new_epoch()
        if last or l == n_layers - 1:
            A.reset()
            fnw = A.f32(D)
            t_fnw = P.tok("fnw")
            P.dma("sync", fnw, fnw_in.partition_broadcast(128), [], [t_fnw])
            xf = [A.f32(D) for _ in range(2)]
            xf_t = [P.tok(f"xf{i}") for i in range(2)]
            jk = A.bf16(D)
            t_jk = P.tok("jk")
            r5 = [A.f32(4) for _ in range(2)]
            r5_t = [P.tok(f"r5{i}") for i in range(2)]
            t_out = P.tok("out")
            for t in range(8):
                b = t % 2
                rows = slice(t * 128, (t + 1) * 128)
                P.dma("sync", xf[b], xs_dst[rows, :], [t_XS[t]], [xf_t[b]])
                P.I("scalar", "activation", [xf_t[b]], [t_jk, r5_t[b]], out=jk, in_=xf[b], func=AF.Square, accum_out=r5[b][:, 0:1])
                P.I("scalar", "activation", [r5_t[b]], [r5_t[b]], out=r5[b][:, 1:2], in_=r5[b][:, 0:1], func=AF.Sqrt, scale=1.0 / D, bias=EPS)
                P.I("vector", "reciprocal", [r5_t[b]], [r5_t[b]], out=r5[b][:, 1:2], in_=r5[b][:, 1:2])
                P.I("vector", "scalar_tensor_tensor", [xf_t[b], r5_t[b], t_fnw], [xf_t[b]], out=xf[b], in0=xf[b], scalar=r5[b][:, 1:2], in1=fnw, op0=ALU.mult, op1=ALU.mult)
                P.dma("sync", out_ap[rows, :], xf[b], [xf_t[b]], [], partial=[t_out])
            P.barrier()

    es_close = es
    if debug == "A":
        fin = P.tok("fin")
    P.barrier()
    global _LAST_P
    _LAST_P = P
    print("ops", len(P.ops), "semgroups", len(P.grps))
    P.emit()
    es.close()
    return nc


def prep_inputs(inp, nlw=DEPTH):
    f = lambda a: np.ascontiguousarray(np.asarray(a, dtype=np.float32))
    x = f(inp["x"])[0]; ctx = f(inp["ctx"])[0]; c = f(inp["c"])[0]; c_ctx = f(inp["c_ctx"])
    ada_w = f(inp["ada_w"]); ada_b = f(inp["ada_b"])
    cc = np.stack([c.reshape(16, 128).T, c_ctx.reshape(16, 128).T], axis=-1).reshape(128, 32)
    ws = f(inp["gmlp_ws"])
    wsT = np.ascontiguousarray(ws.transpose(0, 3, 1, 2)).reshape(DEPTH, 128, 512)
    bs = f(inp["gmlp_bs"])
    gbs = np.ascontiguousarray(bs.transpose(2, 0, 1)).reshape(128, DEPTH * 4)
    s_ = np.arange(128)[:, None]; t_ = np.arange(128)[None, :]
    same = (s_ // 32 == t_ // 32)
    mid_f = (t_ // 32) * 32 + 15
    mid_b = (t_ // 32) * 32 + 16
    Mq_f = same * ((s_ <= t_).astype(np.float32) - (s_ <= mid_f)); Md_f = same * (s_ > t_); mk_f = same * (s_ <= t_)
    Mq_b = same * ((s_ >= t_).astype(np.float32) - (s_ >= mid_b)); Md_b = same * (s_ < t_); mk_b = same * (s_ >= t_)
    perm = np.zeros((128, 128), np.float32)
    for do in range(128):
        if (do % 64) < 32: perm[do + 32, do] = -1.0
        else: perm[do - 32, do] = 1.0
    sel = np.zeros((128, 128), np.float32)
    sv = np.arange(128)
    for c_ in range(4):
        inch = (sv // 32 == c_)
        sel[:, c_] = inch & (sv % 32 <= 15); sel[:, 4 + c_] = inch
        sel[:, 8 + c_] = inch & (sv % 32 >= 16); sel[:, 12 + c_] = inch
    cmat = np.concatenate([np.eye(128, dtype=np.float32), Mq_f, Md_f, mk_f, Mq_b, Md_b, mk_b, perm, sel], axis=1).astype(np.float32)
    inv = (10000.0 ** (-np.arange(0, 64, 2, dtype=np.float32) / 64)).astype(np.float32)
    common = dict(ctx=ctx, cc=cc, norm_w=f(inp["norm_w"]), w_in=f(inp["w_in"]), w_out=f(inp["w_out"]),
                  gmlp_ln_w=f(inp["gmlp_ln_w"]), gmlp_ln_b=f(inp["gmlp_ln_b"]), gmlp_wsT=wsT, gmlp_bs=gbs,
                  hgrn_lb=f(inp["hgrn_lower_bounds"]).reshape(-1), hgrn_norm_w=f(inp["hgrn_norm_w"]),
                  diff_lambda=f(inp["diff_lambda"]).reshape(DEPTH, 512), diff_subln_w=f(inp["diff_subln_w"]),
                  final_norm_w=f(inp["final_norm_w"]), cmat=cmat)
    maps = []
    for i in range(NCORES):
        m = dict(common)
        m["x"] = np.ascontiguousarray(x[i * LAT:(i + 1) * LAT])
        m["ada_w"] = np.ascontiguousarray(np.concatenate([ada_w[:, :, b * D + i * 256: b * D + (i + 1) * 256] for b in range(3)], axis=2))
        m["ada_b"] = np.ascontiguousarray(np.concatenate([ada_b[:, b * D + i * 256: b * D + (i + 1) * 256] for b in range(3)], axis=1)).reshape(-1)
        pos = np.arange(i * LAT, (i + 1) * LAT)
        rowp = (pos // 64).astype(np.float32); colp = (pos % 64).astype(np.float32)
        d = np.arange(128)
        fr = inv[d % 32]
        ang = np.where((d < 64)[:, None], rowp[None, :] * fr[:, None], colp[None, :] * fr[:, None]).astype(np.float32)
        m["rope"] = np.concatenate([np.cos(ang), np.sin(ang)], axis=1).astype(np.float32)
        oh = np.zeros((128, 8), np.float32); oh[:, i] = 1.0
        m["onehot"] = oh
        maps.append(m)
    if nlw != DEPTH:
        for m in maps:
            m["ada_w"] = np.ascontiguousarray(m["ada_w"][:nlw]); m["ada_b"] = np.ascontiguousarray(m["ada_b"].reshape(DEPTH, 768)[:nlw]).reshape(-1)
            m["w_in"] = np.ascontiguousarray(m["w_in"][:nlw]); m["w_out"] = np.ascontiguousarray(m["w_out"][:nlw])
    return maps


_NC_CACHE = {}


def kernel(**inputs):
    maps = prep_inputs(inputs)
    if "nc" not in _NC_CACHE:
        _NC_CACHE["nc"] = build()
    res = run_bass_kernel_spmd(_NC_CACHE["nc"], maps, core_ids=list(range(NCORES)))
    out = np.concatenate([np.asarray(r["out"]) for r in res.results], axis=0)
    return out.reshape(1, NCORES * LAT, D).astype(np.float32)
```

```python
import bisect
import os
import math
from contextlib import ExitStack

import numpy as np
import concourse.bass as bass
import concourse.mybir as mybir
from concourse.bass_utils import run_bass_kernel_spmd

F32 = mybir.dt.float32
BF16 = mybir.dt.bfloat16
AF = mybir.ActivationFunctionType
ALU = mybir.AluOpType

NCORES = 8
D = 2048
LAT = 1024
CTX = 256
TOK = LAT + CTX
NT = TOK // 128
DEPTH = 4
EPS = 1e-6
GROUP_ORDER = [2, 0, 1, 7, 8, 9, 10, 11, 12, 13, 14, 15, 3, 4, 5, 6]


class SemGrp:
    def __init__(self, name):
        self.name = name
        self.sem = None
        self.ids = []
        self.waiters = []
        self.inc = 16


class Tok:
    def __init__(self, name, grp=None):
        self.name = name
        self.w = []
        self.r = []
        self.grp = grp


class Op:
    __slots__ = ("id", "eng", "fn", "deps", "dma", "grp", "sig", "sigidx", "epoch")


class Prog:
    ENGS = ("tensor", "vector", "scalar", "gpsimd", "sync")

    def __init__(self, nc, es):
        self.nc = nc
        self.es = es
        self.ops = []
        self.grps = []
        self.grp_by_name = {}
        self.local_idx = 0
        self.local_gidx = 0
        self.last = {}
        self.epoch = 0
        self.esems = {}
        self._mk_esem()

    CE = ("tensor", "vector", "scalar", "gpsimd")

    def _mk_esem(self):
        self.esems[self.epoch] = {e: self.es.enter_context(self.nc.semaphore(f"sem_{e}_{self.epoch}")) for e in self.CE}

    def new_epoch(self):
        self.epoch += 1
        self._mk_esem()

    def tok(self, name, grp=None):
        return Tok(name, grp)

    def grp(self, name, inc=16):
        g = self.grp_by_name.get(name)
        if g is None:
            g = SemGrp(name)
            g.inc = inc
            self.grp_by_name[name] = g
        return g

    def cc(self, fn, reads, writes):
        tok = writes[0]
        if tok.grp is None:
            tok.grp = self.grp("cc_" + tok.name, 1)
        return self._add("gpsimd", fn, reads, writes, [], dma_tok=tok)

    def _add(self, eng, fn, reads, writes, partial, dma_tok=None, force_sig=False):
        deps = set()
        for t in reads:
            deps.update(t.w)
        for t in list(writes) + list(partial):
            deps.update(t.w)
            deps.update(t.r)
        op = Op()
        op.id = len(self.ops)
        op.eng = eng
        op.fn = fn
        op.epoch = self.epoch
        op.dma = dma_tok is not None
        op.grp = None
        op.sig = force_sig
        op.sigidx = 0
        if op.dma:
            if dma_tok.grp is None:
                if eng == "gpsimd":
                    dma_tok.grp = self.grp(f"locg{self.local_gidx}")
                    self.local_gidx += 1
                else:
                    dma_tok.grp = self.grp(f"loc{self.local_idx}")
                    self.local_idx += 1
            g = dma_tok.grp
            if g.sem is None:
                g.sem = self.es.enter_context(self.nc.semaphore("d_" + g.name))
                self.grps.append(g)
            deps.update(g.waiters)
            g.waiters = []
            g.ids.append(op.id)
            op.grp = g
        latest = {}
        pruned = set()
        for d_ in deps:
            dep = self.ops[d_]
            if dep.dma:
                pruned.add(d_)
            elif latest.get(dep.eng, -1) < d_:
                latest[dep.eng] = d_
        pruned.update(latest.values())
        deps = pruned
        op.deps = deps
        for d in deps:
            dep = self.ops[d]
            if dep.dma:
                dep.grp.waiters.append(op.id)
        self.ops.append(op)
        if fn is not None:
            self.last[eng if not op.dma else ("dma", id(op.grp))] = op.id
        for t in reads:
            t.r.append(op.id)
        for t in writes:
            t.w = [op.id]
            t.r = []
        for t in partial:
            t.w.append(op.id)
            t.r = []
        return op.id

    def I(self, eng, meth, reads, writes, partial=(), **kw):
        nc = self.nc
        return self._add(eng, lambda: getattr(getattr(nc, eng), meth)(**kw), reads, writes, partial)

    def dma(self, queue, out, in_, reads, writes, partial=(), **kw):
        nc = self.nc
        tok = (list(writes) + list(partial))[0]
        return self._add(queue, lambda: getattr(nc, queue).dma_start(out=out, in_=in_, **kw), reads, writes, partial, dma_tok=tok)

    def barrier(self):
        deps = set(self.last.values())
        for e in self.ENGS:
            op = Op()
            op.id = len(self.ops)
            op.eng = e
            op.fn = None
            op.epoch = self.epoch
            op.dma = False
            op.grp = None
            op.sig = False
            op.sigidx = 0
            op.deps = set(deps)
            self.ops.append(op)
        for g in self.grps:
            g.waiters = []
        self.local_idx = 0
        self.local_gidx = 0

    def emit(self):
        nc = self.nc
        ops = self.ops
        for op in ops:
            for d in op.deps:
                dep = ops[d]
                if not dep.dma and not (dep.eng == op.eng and dep.eng == "tensor"):
                    dep.sig = True
        sigcount = {}
        waited = {e: {} for e in self.ENGS}
        for op in ops:
            E = getattr(nc, op.eng)
            need = {}
            for d in op.deps:
                dep = ops[d]
                if dep.dma:
                    g = dep.grp
                    val = g.inc * bisect.bisect_left(g.ids, op.id)
                    key = ("d", id(g))
                    sem = g.sem
                else:
                    if dep.eng == op.eng and op.eng == "tensor":
                        continue
                    if dep.epoch < op.epoch:
                        continue
                    val = dep.sigidx
                    key = ("e", dep.eng, dep.epoch)
                    sem = self.esems[dep.epoch][dep.eng]
                if need.get(key, (None, 0))[1] < val:
                    need[key] = (sem, val)
            for key, (sem, val) in need.items():
                if waited[op.eng].get(key, 0) >= val:
                    continue
                E.wait_ge(sem, val)
                waited[op.eng][key] = val
            if op.fn is None:
                continue
            ins = op.fn()
            if op.dma:
                ins.then_inc(op.grp.sem, op.grp.inc)
            elif op.sig:
                k_ = (op.eng, op.epoch)
                sigcount[k_] = sigcount.get(k_, 0) + 1
                op.sigidx = sigcount[k_]
                ins.then_inc(self.esems[op.epoch][op.eng], 1)
        print("max engine sem counts", {k: v for k, v in sigcount.items() if v > 1000})


class Arena:
    def __init__(self, ap):
        self.ap = ap
        self.n = ap.shape[1]
        self.off = 0

    def reset(self):
        self.off = 0

    def f32(self, n):
        assert self.off + n <= self.n, ("arena overflow", self.off, n, self.n)
        a = self.ap[:, self.off:self.off + n]
        self.off += n
        return a

    def bf16(self, n):
        w = (n + 1) // 2
        assert self.off + w <= self.n, ("arena overflow", self.off, w, self.n)
        a = self.ap[:, self.off:self.off + w].bitcast(BF16)
        self.off += w
        return a


def build(n_layers=DEPTH, debug=False):
    nc = bass.Bass("TRN2", target_bir_lowering=False)
    es = ExitStack()
    P = Prog(nc, es)

    def din(name, shape, dt=F32):
        return nc.dram_tensor(name, list(shape), dt, kind="ExternalInput").ap()

    def dscr(name, shape, dt=F32, out=False):
        if out:
            return nc.dram_tensor(name, list(shape), dt, kind="ExternalOutput").ap()
        return nc.dram_tensor(name, list(shape), dt).ap()

    x_in = din("x", [LAT, D])
    ctx_in = din("ctx", [CTX, D])
    cc_in = din("cc", [128, 32])
    NLW = n_layers if debug else DEPTH
    adaw_in = din("ada_w", [NLW, D, 768])
    adab_in = din("ada_b", [NLW * 768])
    normw_in = din("norm_w", [DEPTH, D])
    win_in = din("w_in", [NLW, D, 8192])
    wout_in = din("w_out", [NLW, D, D])
    glnw_in = din("gmlp_ln_w", [DEPTH, 512])
    glnb_in = din("gmlp_ln_b", [DEPTH, 512])
    gwsT_in = din("gmlp_wsT", [DEPTH, 128, 512])
    gbs_in = din("gmlp_bs", [128, DEPTH * 4])
    hlb_in = din("hgrn_lb", [2 * DEPTH * 512])
    hnw_in = din("hgrn_norm_w", [DEPTH, 128])
    dlam_in = din("diff_lambda", [DEPTH, 512])
    dsub_in = din("diff_subln_w", [DEPTH, 256])
    fnw_in = din("final_norm_w", [D])
    cmat_in = din("cmat", [128, 9 * 128])
    rope_in = din("rope", [128, 2 * LAT])
    oneh_in = din("onehot", [128, 8])
    out_ap = nc.dram_tensor("out", [LAT, D], F32, kind="ExternalOutput").ap()

    modloc = dscr("modloc", [NLW * 6, 256])
    modgat = dscr("modgat", [NCORES * NLW * 6, 256])
    YA = dscr("YA", [TOK, 512], BF16, out=bool(debug))
    YB = dscr("YB", [TOK, 512], BF16, out=bool(debug))
    YC = dscr("YC", [TOK, 1024], BF16, out=bool(debug))
    ZB = dscr("ZB", [TOK, 512], BF16)
    ZC = dscr("ZC", [TOK, 1024], BF16)
    HQ = dscr("HQ", [TOK, 512], BF16)
    HV = dscr("HV", [TOK, 512], BF16)
    HG = dscr("HG", [2, TOK, 512], F32, out=bool(debug))
    HK = dscr("HK", [2, TOK, 512], BF16)
    OFB = dscr("OFB", [2, TOK, 512], F32, out=bool(debug))
    QTs = dscr("QTs", [8 * 128, TOK], BF16)
    KTloc = dscr("KTloc", [8 * 128, LAT], BF16)
    KTc = dscr("KTc", [8 * 128, CTX], BF16)
    Vloc = dscr("Vloc", [LAT, 1032], BF16)
    Vcx = dscr("Vcx", [CTX, 1032], BF16)
    KTgat = dscr("KTgat", [NCORES * 1024, LAT], BF16)
    Vgat = dscr("Vgat", [NCORES * LAT, 1032], BF16)
    QHAT = dscr("QHAT", [2 * 8 * 128, 512], BF16)
    HSloc = dscr("HSloc", [256, 516])
    HSgat = dscr("HSgat", [NCORES * 256, 516])
    SCTX = dscr("SCTX", [256, 512])
    XS = dscr("XS", [2, TOK, D])

    cm = es.enter_context(nc.sbuf_tensor("cm_sb", [128, 9 * 128], BF16))
    rope = es.enter_context(nc.sbuf_tensor("rope_sb", [128, 2 * LAT], F32))
    oneh = es.enter_context(nc.sbuf_tensor("oneh_sb", [128, 8], F32))
    small = es.enter_context(nc.sbuf_tensor("small_sb", [128, 256], F32))
    arena_t = es.enter_context(nc.sbuf_tensor("arena_sb", [128, 48000], F32))
    A = Arena(arena_t[:, :])
    psT = [es.enter_context(nc.psum_tensor(f"psT{i}", [128, 1024], BF16)) for i in range(2)]
    ps = [es.enter_context(nc.psum_tensor(f"ps{i}", [128, 512], F32)) for i in range(6)]
    psT_t = [P.tok(f"psT{i}") for i in range(2)]
    ps_t = [P.tok(f"ps{i}") for i in range(6)]

    ident = cm[:, 0:128]
    Mq = [cm[:, 128:256], cm[:, 512:640]]
    Mdec = [cm[:, 256:384], cm[:, 640:768]]
    Mmask = [cm[:, 384:512], cm[:, 768:896]]
    perm = cm[:, 896:1024]
    selv = [cm[:, 1024:1032], cm[:, 1032:1040]]
    t_const = P.tok("const", P.grp("const"))

    P.dma("gpsimd", cm[:, :], cmat_in, [], [t_const])
    t_rope = P.tok("rope", P.grp("rope"))
    P.dma("sync", rope[:, :], rope_in, [], [t_rope])
    rmask = small[:, 64:68]
    t_rmask = P.tok("rmask", P.grp("rmask"))
    P.dma("sync", rmask, cmat_in[:, 1024 + 4:1024 + 8], [], [t_rmask])
    t_oneh = P.tok("oneh", P.grp("oneh"))
    P.dma("sync", oneh[:, :], oneh_in, [], [t_oneh])

    A.reset()
    cc = A.f32(32)
    t_cc = P.tok("cc")
    P.dma("sync", cc, cc_in, [], [t_cc])
    scc = A.f32(32)
    t_scc = P.tok("scc")
    P.I("scalar", "activation", [t_cc], [t_scc], out=scc, in_=cc, func=AF.Silu)
    adab = A.f32(NLW * 768)
    t_adab = P.tok("adab")
    P.dma("sync", adab[0:2, :], adab_in.partition_broadcast(2), [], [t_adab])
    wa = [A.f32(768) for _ in range(3)]
    wa_t = [P.tok(f"wa{i}") for i in range(3)]
    mo = [A.f32(768) for _ in range(2)]
    mo_t = [P.tok(f"mo{i}") for i in range(2)]
    t_modloc = P.tok("modloc", P.grp("modloc"))
    sccv = scc.rearrange("p (k s) -> p k s", s=2)
    n = 0
    for l in range(NLW):
        pA, pB = ps[0], ps[1]
        for kc in range(16):
            sl = n % 3
            n += 1
            P.dma("sync", wa[sl], adaw_in[l, kc * 128:(kc + 1) * 128, :], [], [wa_t[sl]])
            P.I("tensor", "matmul", [wa_t[sl], t_scc], [], partial=[ps_t[0]], out=pA[0:2, 0:512], lhsT=sccv[:, kc, :], rhs=wa[sl][:, 0:512], start=(kc == 0), stop=(kc == 15))
            P.I("tensor", "matmul", [wa_t[sl], t_scc], [], partial=[ps_t[1]], out=pB[0:2, 0:256], lhsT=sccv[:, kc, :], rhs=wa[sl][:, 512:768], start=(kc == 0), stop=(kc == 15))
        m = mo[l % 2]
        mt = mo_t[l % 2]
        P.I("vector", "tensor_tensor", [ps_t[0], t_adab], [mt], out=m[0:2, 0:512], in0=pA[0:2, 0:512], in1=adab[0:2, l * 768:l * 768 + 512], op=ALU.add)
        P.I("vector", "tensor_tensor", [ps_t[1], t_adab], [], partial=[mt], out=m[0:2, 512:768], in0=pB[0:2, 0:256], in1=adab[0:2, l * 768 + 512:(l + 1) * 768], op=ALU.add)
        dst = modloc.rearrange("(l s b) c -> l s b c", s=2, b=3)[l]
        P.dma("sync", dst, m[0:2, :].rearrange("s (b c) -> s b c", c=256), [mt], [], partial=[t_modloc])
    t_modgat = P.tok("modgat")
    P.cc(lambda: nc.gpsimd.collective_compute("AllGather", ALU.bypass, replica_groups=[list(range(NCORES))], ins=[modloc.opt()], outs=[modgat.opt()]), [t_modloc], [t_modgat])
    P.barrier()
    P.new_epoch()
    if debug == "0":
        dbg0 = nc.dram_tensor("dbg0", [NCORES * NLW * 6, 256], F32, kind="ExternalOutput").ap()
        t_dbg = P.tok("dbg0")
        P.dma("sync", dbg0, modgat, [t_modgat], [t_dbg])
        n_layers = 0

    def modvec(l, s, b):
        return modgat.rearrange("(r q) c -> q r c", q=NLW * 6)[(l * 2 + s) * 3 + b]

    def bcast(ap2d):
        return ap2d.partition_broadcast(128)

    for l in range(n_layers):
        xs_src = None if l == 0 else XS[(l % 2)]
        xs_dst = XS[((l + 1) % 2)]
        lam_init = 0.8 - 0.6 * math.exp(-0.3 * l)

        def xrows(t):
            if l == 0:
                return x_in[t * 128:(t + 1) * 128, :] if t < 8 else ctx_in[(t - 8) * 128:(t - 7) * 128, :]
            return xs_src[t * 128:(t + 1) * 128, :]

        A.reset()
        hT = A.bf16(16 * TOK).rearrange("p (k t) -> p k t", t=TOK)
        t_hT = [P.tok(f"hT{t}") for t in range(NT)]
        Wb = [A.bf16(16 * 512).rearrange("p (k n) -> p k n", n=512) for _ in range(2)]
        Wb_t = [P.tok(f"W{i}") for i in range(2)]
        G = A.f32(D)
        Sh = A.f32(D)
        t_G = P.tok("G")
        t_Sh = P.tok("Sh")
        nwb = A.f32(D)
        t_nwb = P.tok("nwb")
        xt = [A.f32(D) for _ in range(2)]
        xt_t = [P.tok(f"xt{i}") for i in range(2)]
        xn = [A.bf16(D) for _ in range(2)]
        xn_t = [P.tok(f"xn{i}") for i in range(2)]
        st = small
        t_st = [P.tok(f"st{i}") for i in range(2)]

        P.dma("sync", nwb, bcast(normw_in[l:l + 1, :]).rearrange("p o d -> p (o d)") if False else normw_in[l, :].partition_broadcast(128), [], [t_nwb])
        for t in range(NT):
            if t == 0 or t == 8:
                s = 0 if t == 0 else 1
                P.dma("sync", G.rearrange("p (r c) -> p r c", c=256), modvec(l, s, 1).partition_broadcast(128), [t_modgat], [t_G])
                P.dma("sync", Sh.rearrange("p (r c) -> p r c", c=256), modvec(l, s, 0).partition_broadcast(128), [t_modgat], [t_Sh])
                P.I("vector", "scalar_tensor_tensor", [t_G, t_nwb], [t_G], out=G, in0=G, scalar=1.0, in1=nwb, op0=ALU.add, op1=ALU.mult)
            b = t % 2
            P.dma("sync", xt[b], xrows(t), [], [xt_t[b]])
            ss = st[:, b * 4:b * 4 + 1]
            rs = st[:, b * 4 + 1:b * 4 + 2]
            P.I("scalar", "activation", [xt_t[b]], [xn_t[b], t_st[b]], out=xn[b], in_=xt[b], func=AF.Square, accum_out=ss)
            P.I("scalar", "activation", [t_st[b]], [t_st[b]], out=rs, in_=ss, func=AF.Sqrt, scale=1.0 / D, bias=EPS)
            P.I("vector", "reciprocal", [t_st[b]], [t_st[b]], out=rs, in_=rs)
            P.I("vector", "scalar_tensor_tensor", [xt_t[b], t_st[b], t_G], [xt_t[b]], out=xt[b], in0=xt[b], scalar=rs, in1=G, op0=ALU.mult, op1=ALU.mult)
            P.I("gpsimd", "tensor_tensor", [xt_t[b], t_Sh], [xn_t[b]], out=xn[b], in0=xt[b], in1=Sh, op=ALU.add)
            for half in range(2):
                pt = psT[half]
                for j in range(8):
                    kc = half * 8 + j
                    P.I("tensor", "transpose", [xn_t[b], t_const], [], partial=[psT_t[half]], out=pt[:, j * 128:(j + 1) * 128], in_=xn[b][:, kc * 128:(kc + 1) * 128], identity=ident)
                dsth = hT[:, half * 8:(half + 1) * 8, t * 128:(t + 1) * 128]
                srch = pt[:, :].rearrange("p (k t) -> p k t", t=128)
                if half == 0:
                    P.I("vector", "tensor_copy", [psT_t[half]], [], partial=[t_hT[t]], out=dsth, in_=srch)
                else:
                    P.I("scalar", "copy", [psT_t[half]], [], partial=[t_hT[t]], out=dsth, in_=srch)

        if debug == "A1":
            dbgh = nc.dram_tensor("dbgh", [128, 16 * TOK], BF16, kind="ExternalOutput").ap()
            t_dbgh = P.tok("dbgh")
            P.dma("sync", dbgh, hT.rearrange("p k t -> p (k t)"), t_hT, [t_dbgh])
            break
        lnw = A.f32(512)
        lnb = A.f32(512)
        t_ln = P.tok("ln")
        P.dma("sync", lnw, glnw_in[l, :].partition_broadcast(128), [], [t_ln])
        P.dma("sync", lnb, glnb_in[l, :].partition_broadcast(128), [], [], partial=[t_ln])
        wsT = A.bf16(512)
        t_wsT = P.tok("wsT")
        P.dma("gpsimd", wsT, gwsT_in[l], [], [t_wsT])
        gbs = A.f32(DEPTH * 4)
        t_gbs = P.tok("gbs")
        P.dma("sync", gbs, gbs_in, [], [t_gbs])
        lbraw = A.f32(2 * DEPTH * 512)
        t_lb = P.tok("lb")
        P.dma("sync", lbraw, hlb_in.partition_broadcast(128), [], [t_lb])
        lbe = lbraw.rearrange("p (d l c) -> p d l c", d=2, l=DEPTH)
        P.I("scalar", "activation", [t_lb], [t_lb], out=lbraw, in_=lbraw, func=AF.Exp)
        lbs = A.f32(1024).rearrange("p (d c) -> p d c", d=2)
        lbv = A.f32(1024).rearrange("p (d c) -> p d c", d=2)
        oml = A.f32(1024).rearrange("p (d c) -> p d c", d=2)
        t_lbv = P.tok("lbv")
        P.I("vector", "tensor_tensor", [t_lb], [t_lbv], out=lbs, in0=lbe[:, :, 0, :], in1=lbe[:, :, 1, :], op=ALU.add)
        P.I("vector", "tensor_tensor", [t_lb, t_lbv], [t_lbv], out=lbs, in0=lbs, in1=lbe[:, :, 2, :], op=ALU.add)
        P.I("vector", "tensor_tensor", [t_lb, t_lbv], [t_lbv], out=lbs, in0=lbs, in1=lbe[:, :, 3, :], op=ALU.add)
        P.I("vector", "reciprocal", [t_lbv], [t_lbv], out=lbs, in_=lbs)
        if l == 0:
            P.I("vector", "memset", [t_lbv], [], partial=[t_lbv], ap=lbv, constant=0.0)
        else:
            P.I("vector", "tensor_copy", [t_lb, t_lbv], [], partial=[t_lbv], out=lbv, in_=lbe[:, :, 1, :])
            for j in range(2, l + 1):
                P.I("vector", "tensor_tensor", [t_lb, t_lbv], [t_lbv], out=lbv, in0=lbv, in1=lbe[:, :, j, :], op=ALU.add)
            P.I("vector", "tensor_tensor", [t_lbv], [t_lbv], out=lbv, in0=lbv, in1=lbs, op=ALU.mult)
        P.I("vector", "tensor_scalar", [t_lbv], [t_lbv], out=oml, in0=lbv, scalar1=-1.0, scalar2=1.0, op0=ALU.mult, op1=ALU.add)

        UZ = A.bf16(NT * 512).rearrange("p (t c) -> p t c", c=512)
        t_UZ = [P.tok(f"UZ{t}") for t in range(NT)]
        NS = 3
        sf = [A.f32(512) for _ in range(NS)]
        sf_t = [P.tok(f"sf{i}") for i in range(NS)]
        sf2 = [A.f32(512) for _ in range(NS)]
        sf2_t = [P.tok(f"sf2{i}") for i in range(NS)]
        sb = [A.bf16(512) for _ in range(NS)]
        sb_t = [P.tok(f"sb{i}") for i in range(NS)]
        sv = [A.bf16(2 * 258) for _ in range(NS)]
        sv_t = [P.tok(f"sv{i}") for i in range(NS)]
        for i in range(NS):
            P.I("gpsimd", "memset", [], [sv_t[i]], ap=sv[i], constant=1.0)
        bst = [A.f32(16) for _ in range(NS)]
        bst_t = [P.tok(f"bst{i}") for i in range(NS)]
        vnb = [A.bf16(512) for _ in range(NS)]
        vnb_t = [P.tok(f"vnb{i}") for i in range(NS)]
        def dtoks(name, n):
            g = P.grp(name)
            return [P.tok(f"{name}{i}", g) for i in range(n)]
        t_YA = dtoks("YA", NT)
        t_ZB = dtoks("ZB", NT)
        t_ZC = dtoks("ZC", NT)
        t_HQ = dtoks("HQ", NT)
        t_HV = dtoks("HV", NT)
        t_HG = dtoks("HG", 2 * NT)
        t_HK = dtoks("HK", 2 * NT)
        t_QTs = dtoks("QTs", 8)
        t_KTloc = P.tok("KTloc", P.grp("KTloc"))
        t_KTc = P.tok("KTc", P.grp("KTc"))
        t_Vloc = P.tok("Vloc", P.grp("Vloc"))
        t_Vcx = P.tok("Vcx", P.grp("Vcx"))

        cnt = [0]

        def slot():
            cnt[0] += 1
            return cnt[0] % NS

        gmlp_pending = []
        wi = 0
        gorder = GROUP_ORDER
        if isinstance(debug, str) and debug.startswith("A:"):
            gorder = [int(v) for v in debug[2:].split(",")]
        def load_w(gi_):
            g_ = gorder[gi_]
            for q4 in range(4):
                P.dma("gpsimd", Wb[gi_ % 2][:, q4 * 4:(q4 + 1) * 4, :], win_in[l, q4 * 512:(q4 + 1) * 512, g_ * 512:(g_ + 1) * 512].rearrange("(k p) n -> p k n", p=128), [], [Wb_t[gi_ % 2]] if q4 == 0 else [], partial=[] if q4 == 0 else [Wb_t[gi_ % 2]])
        load_w(0)
        for gi, g in enumerate(gorder):
            wb = Wb[gi % 2]
            wt = Wb_t[gi % 2]
            if g in (8, 9, 10, 11):
                isq = g in (8, 9)
                for j in range(4):
                    if j == 2 and gi + 1 < len(gorder):
                        load_w(gi + 1)
                    blk = (g - (8 if isq else 10)) * 4 + j
                    for ch, (t0, tn) in enumerate(((0, 512), (512, 512), (1024, 256))):
                        pb = (blk * 3 + ch) % 4
                        pp = ps[pb]
                        tiles = list(range(t0 // 128, (t0 + tn) // 128))
                        for kc in range(16):
                            P.I("tensor", "matmul", [wt] + [t_hT[t] for t in tiles], [], partial=[ps_t[pb]], out=pp[:, 0:tn], lhsT=wb[:, kc, j * 128:(j + 1) * 128], rhs=hT[:, kc, t0:t0 + tn], start=(kc == 0), stop=(kc == 15))
                        s = slot()
                        if ch < 2 and not os.environ.get('KDBG_NOROPE'):
                            P.I("scalar", "copy", [ps_t[pb]], [sb_t[s]], out=sb[s], in_=pp[:, 0:512])
                            pr = ps[4 + (ch % 2)]
                            prt = ps_t[4 + (ch % 2)]
                            kd = os.environ.get("KDBG_ROPE", "")
                            if "nomm" not in kd:
                                P.I("tensor", "matmul", [sb_t[s], t_const], [prt], out=pr[:, 0:512], lhsT=perm, rhs=sb[s], start=True, stop=True)
                            else:
                                pr, prt = pp, ps_t[pb]
                            if "nodve" not in kd:
                                P.I("vector", "tensor_tensor", [ps_t[pb], t_rope, sb_t[s]], [sf_t[s]], out=sf[s], in0=pp[:, 0:512], in1=rope[:, t0:t0 + 512], op=ALU.mult)
                                P.I("vector", "tensor_tensor", [prt, t_rope], [sf2_t[s]], out=sf2[s], in0=pr[:, 0:512], in1=rope[:, LAT + t0:LAT + t0 + 512], op=ALU.mult)
                            else:
                                P.I("scalar", "copy", [ps_t[pb]], [sf_t[s]], out=sf[s], in_=pp[:, 0:512])
                                P.I("scalar", "copy", [prt], [sf2_t[s]], out=sf2[s], in_=pr[:, 0:512])
                            s2 = slot()
                            P.I("vector" if "nogps" in kd else "gpsimd", "tensor_tensor", [sf_t[s], sf2_t[s]], [sb_t[s2]], out=sb[s2], in0=sf[s], in1=sf2[s], op=ALU.add)
                            if isq:
                                P.dma("sync", QTs[blk * 128:(blk + 1) * 128, t0:t0 + 512], sb[s2], [sb_t[s2]], [], partial=[t_QTs[blk]])
                            else:
                                P.dma("sync", KTloc[blk * 128:(blk + 1) * 128, t0:t0 + 512], sb[s2], [sb_t[s2]], [], partial=[t_KTloc])
                        else:
                            P.I("scalar", "copy", [ps_t[pb]], [sb_t[s]], out=sb[s][:, 0:256], in_=pp[:, 0:256])
                            if isq:
                                P.dma("sync", QTs[blk * 128:(blk + 1) * 128, 1024:1280], sb[s][:, 0:256], [sb_t[s]], [], partial=[t_QTs[blk]])
                            else:
                                P.dma("sync", KTc[blk * 128:(blk + 1) * 128, :], sb[s][:, 0:256], [sb_t[s]], [], partial=[t_KTc])
                continue
            order = list(range(NT))
            for t in order:
                if t == 5 and gi + 1 < len(gorder):
                    load_w(gi + 1)
                pb = wi % 4
                wi += 1
                pp = ps[pb]
                for kc in range(16):
                    P.I("tensor", "matmul", [wt, t_hT[t]], [], partial=[ps_t[pb]], out=pp[:, :], lhsT=hT[:, kc, t * 128:(t + 1) * 128], rhs=wb[:, kc, :], start=(kc == 0), stop=(kc == 15))
                rows = slice(t * 128, (t + 1) * 128)
                if g == 2:
                    P.I("scalar", "activation", [ps_t[pb]], [t_UZ[t]], out=UZ[:, t, :], in_=pp[:, :], func=AF.Silu)
                elif g == 0:
                    s = slot()
                    P.I("scalar", "activation", [ps_t[pb]], [sf_t[s]], out=sf[s], in_=pp[:, :], func=AF.Gelu)
                    P.I("gpsimd", "tensor_tensor", [sf_t[s], t_UZ[t]], [t_UZ[t]], out=UZ[:, t, :], in0=sf[s], in1=UZ[:, t, :], op=ALU.mult)
                elif g == 1:
                    s = slot()
                    P.I("scalar", "activation", [ps_t[pb]], [sf_t[s]], out=sf[s], in_=pp[:, :], func=AF.Gelu)
                    bs6 = bst[s][:, 0:6]
                    mv = bst[s][:, 8:10]
                    P.I("vector", "bn_stats", [sf_t[s]], [bst_t[s]], out=bs6, in_=sf[s])
                    P.I("vector", "bn_aggr", [bst_t[s]], [bst_t[s]], out=mv, in_=bs6)
                    rstd = bst[s][:, 10:11]
                    P.I("scalar", "activation", [bst_t[s]], [bst_t[s]], out=rstd, in_=bst[s][:, 9:10], func=AF.Sqrt, bias=EPS)
                    P.I("vector", "reciprocal", [bst_t[s]], [bst_t[s]], out=rstd, in_=rstd)
                    P.I("vector", "tensor_scalar", [sf_t[s], bst_t[s]], [sf_t[s]], out=sf[s], in0=sf[s], scalar1=bst[s][:, 8:9], scalar2=rstd, op0=ALU.subtract, op1=ALU.mult)
                    P.I("vector", "tensor_tensor", [sf_t[s], t_ln], [sf_t[s]], out=sf[s], in0=sf[s], in1=lnw, op=ALU.mult)
                    P.I("gpsimd", "tensor_tensor", [sf_t[s], t_ln], [vnb_t[s]], out=vnb[s], in0=sf[s], in1=lnb, op=ALU.add)
                    gmlp_pending.append((t, s))
                    if len(gmlp_pending) > 1 or t == order[-1]:
                        while gmlp_pending and (len(gmlp_pending) > 1 or t == order[-1]):
                            tt, ss_ = gmlp_pending.pop(0)
                            pm = ps[4 + (tt % 2)]
                            pmt = ps_t[4 + (tt % 2)]
                            for h in range(4):
                                P.I("tensor", "matmul", [vnb_t[ss_], t_wsT], [], partial=[pmt], out=pm[:, h * 128:(h + 1) * 128], lhsT=wsT[:, h * 128:(h + 1) * 128], rhs=vnb[ss_][:, h * 128:(h + 1) * 128], start=True, stop=True)
                            s3 = slot()
                            for h in range(4):
                                P.I("vector", "scalar_tensor_tensor", [pmt, t_gbs, t_UZ[tt]], [], partial=[sb_t[s3]], out=sb[s3][:, h * 128:(h + 1) * 128], in0=pm[:, h * 128:(h + 1) * 128], scalar=gbs[:, l * 4 + h:l * 4 + h + 1], in1=UZ[:, tt, h * 128:(h + 1) * 128], op0=ALU.add, op1=ALU.mult)
                            P.dma("sync", YA[tt * 128:(tt + 1) * 128, :], sb[s3], [sb_t[s3]], [t_YA[tt]])
                elif g in (7, 14, 15, 3):
                    s = slot()
                    P.I("scalar", "activation", [ps_t[pb]], [sb_t[s]], out=sb[s], in_=pp[:, :], func=AF.Silu)
                    if g == 7:
                        P.dma("sync", ZB[rows, :], sb[s], [sb_t[s]], [t_ZB[t]])
                    elif g == 3:
                        P.dma("sync", HQ[rows, :], sb[s], [sb_t[s]], [t_HQ[t]])
                    else:
                        c0 = (g - 14) * 512
                        P.dma("sync", ZC[rows, c0:c0 + 512], sb[s], [sb_t[s]], [] if g == 15 else [t_ZC[t]], partial=[t_ZC[t]] if g == 15 else [])
                elif g == 4:
                    s = slot()
                    P.I("scalar", "copy", [ps_t[pb]], [sb_t[s]], out=sb[s], in_=pp[:, :])
                    P.dma("sync", HV[rows, :], sb[s], [sb_t[s]], [t_HV[t]])
                elif g in (12, 13):
                    s = slot()
                    P.I("vector", "tensor_copy", [ps_t[pb]], [], partial=[sv_t[s]], out=sv[s].rearrange("p (h c) -> p h c", c=258)[:, :, 0:256], in_=pp[:, :].rearrange("p (h c) -> p h c", c=256))
                    c0 = (g - 12) * 516
                    if t < 8:
                        P.dma("sync", Vloc[rows, c0:c0 + 516], sv[s], [sv_t[s]], [], partial=[t_Vloc])
                    else:
                        P.dma("sync", Vcx[(t - 8) * 128:(t - 7) * 128, c0:c0 + 516], sv[s], [sv_t[s]], [], partial=[t_Vcx])
                elif g in (5, 6):
                    d = g - 5
                    s = slot()
                    P.I("scalar", "activation", [ps_t[pb]], [sf_t[s]], out=sf[s], in_=pp[:, :], func=AF.Sigmoid)
                    P.I("vector", "tensor_tensor", [sf_t[s], t_lbv], [sf_t[s]], out=sf[s], in0=sf[s], in1=oml[:, d, :], op=ALU.mult)
                    P.I("vector", "tensor_tensor", [sf_t[s], t_lbv], [sf_t[s]], out=sf[s], in0=sf[s], in1=lbv[:, d, :], op=ALU.add)
                    P.I("scalar", "activation", [sf_t[s]], [sf2_t[s]], out=sf2[s], in_=sf[s], func=AF.Ln)
                    P.I("gpsimd", "tensor_scalar", [sf_t[s]], [sb_t[s]], out=sb[s], in0=sf[s], scalar1=-1.0, scalar2=1.0, op0=ALU.mult, op1=ALU.add)
                    P.dma("sync", HG[d, rows, :], sf2[s], [sf2_t[s]], [t_HG[d * NT + t]])
                    P.dma("sync", HK[d, rows, :], sb[s], [sb_t[s]], [t_HK[d * NT + t]])
            if g == 13:
                t_KTgat = P.tok("KTgat")
                t_Vgat = P.tok("Vgat")
                P.cc(lambda: nc.gpsimd.collective_compute("AllGather", ALU.bypass, replica_groups=[list(range(NCORES))], ins=[KTloc.opt()], outs=[KTgat.opt()]), [t_KTloc], [t_KTgat])
                P.cc(lambda: nc.gpsimd.collective_compute("AllGather", ALU.bypass, replica_groups=[list(range(NCORES))], ins=[Vloc.opt()], outs=[Vgat.opt()]), [t_Vloc], [t_Vgat])
        P.barrier()
        if isinstance(debug, str) and debug.startswith("A"):
            break
        A.reset()
        t_OFB = dtoks("OFB", 2 * NT)
        t_QHAT = dtoks("QHAT", 16)
        t_HSloc = P.tok("HSloc", P.grp("HSloc"))
        t_SCTX = P.tok("SCTX", P.grp("SCTX"))
        R2 = 2
        gq = [A.bf16(512) for _ in range(R2)]; gq_t = [P.tok(f"gq{i}") for i in range(R2)]
        gv = [A.bf16(512) for _ in range(R2)]; gv_t = [P.tok(f"gv{i}") for i in range(R2)]
        gg = [A.f32(512) for _ in range(R2)]; gg_t = [P.tok(f"gg{i}") for i in range(R2)]
        gk = [A.bf16(512) for _ in range(R2)]; gk_t = [P.tok(f"gk{i}") for i in range(R2)]
        ghi = [A.bf16(512) for _ in range(R2)]; ghi_t = [P.tok(f"ghi{i}") for i in range(R2)]
        glo = [A.bf16(512) for _ in range(R2)]; glo_t = [P.tok(f"glo{i}") for i in range(R2)]
        Eq = [A.f32(512) for _ in range(R2)]; Eq_t = [P.tok(f"Eq{i}") for i in range(R2)]
        Ek = [A.f32(512) for _ in range(R2)]; Ek_t = [P.tok(f"Ek{i}") for i in range(R2)]
        Ed = [A.f32(512) for _ in range(R2)]; Ed_t = [P.tok(f"Ed{i}") for i in range(R2)]
        qt_ = [A.bf16(512) for _ in range(R2)]; qt_t = [P.tok(f"qt{i}") for i in range(R2)]
        kt_ = [A.bf16(512) for _ in range(R2)]; kt_t = [P.tok(f"kt{i}") for i in range(R2)]
        kd_ = [A.bf16(512) for _ in range(R2)]; kd_t = [P.tok(f"kd{i}") for i in range(R2)]
        qT = [A.bf16(512) for _ in range(R2)]; qT_t = [P.tok(f"qT{i}") for i in range(R2)]
        kT = [A.bf16(512) for _ in range(R2)]; kT_t = [P.tok(f"kT{i}") for i in range(R2)]
        qTz = [A.bf16(2048) for _ in range(R2)]; qTz_t = [P.tok(f"qTz{i}") for i in range(R2)]
        PTb = [A.bf16(512) for _ in range(R2)]; PT_t = [P.tok(f"PT{i}") for i in range(R2)]
        craw = [A.f32(32) for _ in range(R2)]; craw_t = [P.tok(f"craw{i}") for i in range(R2)]
        cexp = [A.f32(32) for _ in range(R2)]; cexp_t = [P.tok(f"cexp{i}") for i in range(R2)]
        arg = [A.f32(16) for _ in range(R2)]; arg_t = [P.tok(f"arg{i}") for i in range(R2)]
        qs = [A.f32(16) for _ in range(R2)]
        vm = [A.bf16(512) for _ in range(4)]; vm_t = [P.tok(f"vm{i}") for i in range(4)]; qs_t = [P.tok(f"qs{i}") for i in range(R2)]
        ost = [A.f32(512) for _ in range(R2)]; ost_t = [P.tok(f"ost{i}") for i in range(R2)]
        qh = [A.bf16(512) for _ in range(R2)]; qh_t = [P.tok(f"qh{i}") for i in range(R2)]
        Sst = A.f32(512); t_S = P.tok("S")
        Sbf = [A.bf16(512) for _ in range(2)]; Sbf_t = [P.tok(f"Sbf{i}") for i in range(2)]
        Bacc = A.f32(4); Btmp = A.f32(4); t_B = P.tok("Bacc")
        Dst = A.f32(4); t_Dst = P.tok("Dst")
        for i in range(R2):
            P.I("gpsimd", "memset", [], [qTz_t[i]], ap=qTz[i], constant=0.0)
        psA, psB, psC, psS, psO, psU = ps
        tA, tB, tC, tS, tO, tU = ps_t
        it = 0
        for d in range(2):
            scan = ([8, 9] + list(range(8))) if d == 0 else ([9, 8] + list(range(7, -1, -1)))
            P.I("vector", "memset", [], [t_S], ap=Sst, constant=0.0)
            P.I("vector", "memset", [], [t_B], ap=Bacc, constant=0.0)
            for si, t in enumerate(scan):
                b = it % R2
                it += 1
                lat = t < 8
                rows = slice(t * 128, (t + 1) * 128)
                if si == 2:
                    P.dma("sync", SCTX[d * 128:(d + 1) * 128, :], Sst, [t_S], [], partial=[t_SCTX])
                    P.I("vector", "memset", [t_S], [t_S], ap=Sst, constant=0.0)
                P.dma("sync", gq[b], HQ[rows, :], [t_HQ[t]], [gq_t[b]])
                P.dma("sync", gv[b], HV[rows, :], [t_HV[t]], [gv_t[b]])
                P.dma("sync", gg[b], HG[d, rows, :], [t_HG[d * NT + t]], [gg_t[b]])
                P.dma("sync", gk[b], HK[d, rows, :], [t_HK[d * NT + t]], [gk_t[b]])
                P.I("scalar", "copy", [gg_t[b]], [ghi_t[b]], out=ghi[b], in_=gg[b])
                P.I("gpsimd", "tensor_tensor", [gg_t[b], ghi_t[b]], [glo_t[b]], out=glo[b], in0=gg[b], in1=ghi[b], op=ALU.subtract)
                for (pp_, tp_, M_) in ((psA, tA, Mq[d]), (psB, tB, Mdec[d])):
                    P.I("tensor", "matmul", [ghi_t[b], t_const], [tp_], out=pp_[:, :], lhsT=M_, rhs=ghi[b], start=True, stop=False)
                    P.I("tensor", "matmul", [glo_t[b], t_const], [], partial=[tp_], out=pp_[:, :], lhsT=M_, rhs=glo[b], start=False, stop=True)
                for h in range(4):
                    hs = slice(h * 128, (h + 1) * 128)
                    P.I("tensor", "matmul", [ghi_t[b], t_const], [tC] if h == 0 else [], partial=[] if h == 0 else [tC], out=psC[:, h * 8:(h + 1) * 8], lhsT=ghi[b][:, hs], rhs=selv[d], start=True, stop=False)
                    P.I("tensor", "matmul", [glo_t[b], t_const], [], partial=[tC], out=psC[:, h * 8:(h + 1) * 8], lhsT=glo[b][:, hs], rhs=selv[d], start=False, stop=True)
                P.I("scalar", "activation", [tA], [Eq_t[b]], out=Eq[b], in_=psA[:, :], func=AF.Exp)
                P.I("scalar", "activation", [tA], [Ek_t[b]], out=Ek[b], in_=psA[:, :], func=AF.Exp, scale=-1.0)
                P.I("scalar", "activation", [tB], [Ed_t[b]], out=Ed[b], in_=psB[:, :], func=AF.Exp)
                P.I("scalar", "copy", [tC], [craw_t[b]], out=craw[b], in_=psC[:, 0:32])
                P.I("scalar", "activation", [craw_t[b]], [cexp_t[b]], out=cexp[b], in_=craw[b], func=AF.Exp)
                cr = craw[b].rearrange("p (h c) -> p h c", c=8)
                ce = cexp[b].rearrange("p (h c) -> p h c", c=8)
                corder = (0, 1, 2, 3) if d == 0 else (3, 2, 1, 0)
                if lat:
                    av = arg[b].rearrange("p (h c) -> p h c", c=4)
                    for ci_, c in enumerate(corder):
                        P.I("vector", "tensor_tensor", [craw_t[b], t_B], [arg_t[b]] if ci_ == 0 else [], partial=[] if ci_ == 0 else [arg_t[b]], out=av[:, :, c], in0=cr[:, :, c], in1=Bacc, op=ALU.add)
                        P.I("vector", "tensor_tensor", [craw_t[b], t_B], [t_B], out=Bacc, in0=cr[:, :, 4 + c], in1=Bacc, op=ALU.add)
                    P.I("scalar", "activation", [arg_t[b]], [qs_t[b]], out=qs[b], in_=arg[b], func=AF.Exp)
                for c in range(4):
                    P.I("gpsimd", "tensor_scalar", [gv_t[b], t_rmask], [vm_t[c]], out=vm[c], in0=gv[b], scalar1=rmask[:, c:c + 1], scalar2=None, op0=ALU.mult)
                P.I("gpsimd", "tensor_tensor", [gq_t[b], Eq_t[b]], [qt_t[b]], out=qt_[b], in0=gq[b], in1=Eq[b], op=ALU.mult)
                P.I("gpsimd", "tensor_tensor", [gk_t[b], Ek_t[b]], [kt_t[b]], out=kt_[b], in0=gk[b], in1=Ek[b], op=ALU.mult)
                P.I("gpsimd", "tensor_tensor", [gk_t[b], Ed_t[b]], [kd_t[b]], out=kd_[b], in0=gk[b], in1=Ed[b], op=ALU.mult)
                for h in range(4):
                    hs = slice(h * 128, (h + 1) * 128)
                    P.I("tensor", "transpose", [qt_t[b], t_const], [psT_t[0]] if h == 0 else [], partial=[] if h == 0 else [psT_t[0]], out=psT[0][:, hs], in_=qt_[b][:, hs], identity=ident)
                for h in range(4):
                    hs = slice(h * 128, (h + 1) * 128)
                    P.I("tensor", "transpose", [kt_t[b], t_const], [psT_t[1]] if h == 0 else [], partial=[] if h == 0 else [psT_t[1]], out=psT[1][:, hs], in_=kt_[b][:, hs], identity=ident)
                P.I("scalar", "copy", [psT_t[0]], [qT_t[b]], out=qT[b], in_=psT[0][:, 0:512])
                P.I("vector", "tensor_copy", [psT_t[1]], [kT_t[b]], out=kT[b], in_=psT[1][:, 0:512])
                qz = qTz[b].rearrange("p (h x) -> p h x", x=512)
                q3 = qT[b].rearrange("p (h x) -> p h x", x=128)
                for c in range(4):
                    P.I("gpsimd", "tensor_copy", [qT_t[b]], [], partial=[qTz_t[b]], out=qz[:, :, c * 160:c * 160 + 32], in_=q3[:, :, c * 32:(c + 1) * 32])
                if lat:
                    for h in range(4):
                        for c in range(4):
                            P.I("vector", "tensor_scalar", [qT_t[b], qs_t[b]], [], partial=[qh_t[b]], out=qh[b][:, h * 128 + c * 32:h * 128 + (c + 1) * 32], in0=qT[b][:, h * 128 + c * 32:h * 128 + (c + 1) * 32], scalar1=qs[b][:, h * 4 + c:h * 4 + c + 1], scalar2=None, op0=ALU.mult)
                    P.dma("sync", QHAT[(d * 8 + t) * 128:(d * 8 + t + 1) * 128, :], qh[b], [qh_t[b]], [t_QHAT[d * 8 + t]])
                for h in range(4):
                    hs = slice(h * 128, (h + 1) * 128)
                    P.I("tensor", "matmul", [kT_t[b], qT_t[b]], [tS] if h == 0 else [], partial=[] if h == 0 else [tS], out=psS[:, hs], lhsT=kT[b][:, hs], rhs=qT[b][:, hs], start=True, stop=True)
                for h in range(4):
                    hs = slice(h * 128, (h + 1) * 128)
                    P.I("vector", "tensor_tensor", [tS, t_const], [PT_t[b]] if h == 0 else [], partial=[] if h == 0 else [PT_t[b]], out=PTb[b][:, hs], in0=psS[:, hs], in1=Mmask[d], op=ALU.mult)
                for h in range(4):
                    hs = slice(h * 128, (h + 1) * 128)
                    P.I("tensor", "matmul", [PT_t[b], gv_t[b]], [tO] if h == 0 else [], partial=[] if h == 0 else [tO], out=psO[:, hs], lhsT=PTb[b][:, hs], rhs=gv[b][:, hs], start=(h == 0), stop=False)
                for ci_, c in enumerate(corder):
                    sbi = ci_ % 2
                    for h in range(4):
                        hs = slice(h * 128, (h + 1) * 128)
                        P.I("vector", "tensor_scalar", [t_S, cexp_t[b]], [Sbf_t[sbi]] if h == 0 else [], partial=[] if h == 0 else [Sbf_t[sbi]], out=Sbf[sbi][:, hs], in0=Sst[:, hs], scalar1=ce[:, h, c:c + 1], scalar2=None, op0=ALU.mult)
                    for h in range(4):
                        hs = slice(h * 128, (h + 1) * 128)
                        P.I("tensor", "matmul", [qTz_t[b], Sbf_t[sbi]], [], partial=[tO], out=psO[:, hs], lhsT=qz[:, h, c * 128:(c + 1) * 128], rhs=Sbf[sbi][:, hs], start=False, stop=(ci_ == 3 and h == 3))
                    for h in range(4):
                        hs = slice(h * 128, (h + 1) * 128)
                        P.I("tensor", "matmul", [kd_t[b], vm_t[c]], [tU] if h == 0 else [], partial=[] if h == 0 else [tU], out=psU[:, hs], lhsT=kd_[b][:, hs], rhs=vm[c][:, hs], start=True, stop=True)
                    for h in range(4):
                        hs = slice(h * 128, (h + 1) * 128)
                        P.I("vector", "scalar_tensor_tensor", [t_S, cexp_t[b], tU], [], partial=[t_S], out=Sst[:, hs], in0=Sst[:, hs], scalar=ce[:, h, 4 + c:5 + c], in1=psU[:, hs], op0=ALU.mult, op1=ALU.add)
                P.I("scalar", "copy", [tO], [ost_t[b]], out=ost[b], in_=psO[:, :])
                P.dma("sync", OFB[d, rows, :], ost[b], [ost_t[b]], [t_OFB[d * NT + t]])
            P.dma("sync", HSloc[d * 128:(d + 1) * 128, 0:512], Sst, [t_S], [], partial=[t_HSloc])
            P.I("scalar", "activation", [t_B], [t_Dst], out=Dst, in_=Bacc, func=AF.Exp)
            P.dma("sync", HSloc[d * 128:(d + 1) * 128, 512:516], Dst, [t_Dst], [], partial=[t_HSloc])
        t_HSgat = P.tok("HSgat")
        P.cc(lambda: nc.gpsimd.collective_compute("AllGather", ALU.bypass, replica_groups=[list(range(NCORES))], ins=[HSloc.opt()], outs=[HSgat.opt()]), [t_HSloc], [t_HSgat])
        P.barrier()
        if debug == "B":
            break
        A.reset()
        NKT = 66
        KTh = A.bf16(2 * 8448).rearrange("p (m t) -> p m t", m=2)
        Vh = A.bf16(NKT * 258).rearrange("p (j c) -> p j c", c=258)
        kv_t = [P.tok(f"kv{i}") for i in range(9)]
        Qg = [A.bf16(512).rearrange("p (m q) -> p m q", m=2) for _ in range(2)]
        Qg_t = [P.tok(f"Qg{i}") for i in range(2)]
        Eb = [A.bf16(512) for _ in range(3)]
        Eb_t = [P.tok(f"Eb{i}") for i in range(3)]
        lp = A.f32(512)
        t_lp = P.tok("lp")
        lamw = A.f32(256 + 8)
        t_lam = P.tok("lam")
        subw = A.f32(256)
        t_subw = P.tok("subw")
        zc = [A.bf16(256) for _ in range(2)]
        zc_t = [P.tok(f"zc{i}") for i in range(2)]
        od = [A.f32(256) for _ in range(2)]
        od_t = [P.tok(f"od{i}") for i in range(2)]
        tmp1 = [A.f32(256) for _ in range(2)]
        tmp1_t = [P.tok(f"tmp1{i}") for i in range(2)]
        ycb = [A.bf16(256) for _ in range(2)]
        ycb_t = [P.tok(f"ycb{i}") for i in range(2)]
        sq = A.bf16(256)
        t_sq = P.tok("sq")
        rr = [A.f32(8) for _ in range(2)]
        rr_t = [P.tok(f"rr{i}") for i in range(2)]
        t_YC = dtoks("YC", NT)
        P.dma("sync", lp, dlam_in[l, :].partition_broadcast(128), [], [t_lp])
        lpv = lp.rearrange("p (a b c) -> p a b c", a=2, b=2)
        P.I("vector", "tensor_tensor", [t_lp], [t_lam], out=lamw[:, 0:256].rearrange("p (a c) -> p a c", a=2), in0=lpv[:, :, 0, :], in1=lpv[:, :, 1, :], op=ALU.mult)
        P.I("vector", "tensor_reduce", [t_lam], [t_lam], out=lamw[:, 256:258], in_=lamw[:, 0:256].rearrange("p (a c) -> p a c", a=2), axis=mybir.AxisListType.X, op=ALU.add)
        P.I("scalar", "activation", [t_lam], [t_lam], out=lamw[:, 258:260], in_=lamw[:, 256:258], func=AF.Exp)
        P.I("vector", "tensor_tensor", [t_lam], [t_lam], out=lamw[:, 260:261], in0=lamw[:, 258:259], in1=lamw[:, 259:260], op=ALU.subtract)
        P.I("vector", "tensor_scalar", [t_lam], [t_lam], out=lamw[:, 260:261], in0=lamw[:, 260:261], scalar1=float(lam_init), scalar2=None, op0=ALU.add)
        lamcol = lamw[:, 260:261]
        P.dma("sync", subw, dsub_in[l, :].partition_broadcast(128), [], [t_subw])
        P.I("vector", "tensor_scalar", [t_subw], [t_subw], out=subw, in0=subw, scalar1=float(1.0 - lam_init), scalar2=None, op0=ALU.mult)
        acc = [[ps[2], ps[3]], [ps[4], ps[5]]]
        acc_t = [[ps_t[2], ps_t[3]], [ps_t[4], ps_t[5]]]
        SCALE = float(128 ** -0.5)
        uq = 0
        ue = 0
        uo = 0
        for h in range(4):
            for r_ in range(8):
                P.dma("sync", KTh[:, :, r_ * 1024:(r_ + 1) * 1024], KTgat[r_ * 1024 + h * 256:r_ * 1024 + (h + 1) * 256, :].rearrange("(m d) t -> d m t", d=128), [t_KTgat], [kv_t[r_]])
                P.dma("sync", Vh[:, r_ * 8:(r_ + 1) * 8, :], Vgat[r_ * 1024:(r_ + 1) * 1024, h * 258:(h + 1) * 258].rearrange("(j p) c -> p j c", p=128), [t_Vgat], [], partial=[kv_t[r_]])
            P.dma("sync", KTh[:, :, 8192:8448], KTc[h * 256:(h + 1) * 256, :].rearrange("(m d) t -> d m t", d=128), [t_KTc], [kv_t[8]])
            P.dma("sync", Vh[:, 64:66, :], Vcx[:, h * 258:(h + 1) * 258].rearrange("(j p) c -> p j c", p=128), [t_Vcx], [], partial=[kv_t[8]])
            def load_q(qg_):
                P.dma("sync", Qg[qg_ % 2], QTs[h * 256:(h + 1) * 256, qg_ * 256:(qg_ + 1) * 256].rearrange("(m d) q -> d m q", d=128), [t_QTs[2 * h], t_QTs[2 * h + 1]], [Qg_t[qg_ % 2]])
            load_q(0)
            for qg in range(5):
                q0 = qg * 256
                kts = list(range(NKT)) if qg < 4 else [64, 65]
                qb = qg % 2

                def qk(ki_):
                    kt_ = kts[ki_]
                    sbk_ = ki_ % 2
                    kvt_ = kv_t[min(kt_ // 8, 8)]
                    for m in range(2):
                        P.I("tensor", "matmul", [kvt_, Qg_t[qb]], [ps_t[sbk_]] if m == 0 else [], partial=[] if m == 0 else [ps_t[sbk_]], out=ps[sbk_][:, m * 256:(m + 1) * 256], lhsT=KTh[:, m, kt_ * 128:(kt_ + 1) * 128], rhs=Qg[qb][:, m, :], start=True, stop=True)
                qk(0)
                for ki, kt in enumerate(kts):
                    sbk = ki % 2
                    pS = ps[sbk]
                    kvt = kv_t[min(kt // 8, 8)]
                    if ki + 1 < len(kts):
                        qk(ki + 1)
                    eb = ue % 3
                    ue += 1
                    P.I("scalar", "activation", [ps_t[sbk]], [Eb_t[eb]], out=Eb[eb], in_=pS[:, :], func=AF.Exp, scale=SCALE)
                    for m in range(2):
                        for qi in range(2):
                            P.I("tensor", "matmul", [Eb_t[eb], kvt], [acc_t[m][qi]] if ki == 0 else [], partial=[] if ki == 0 else [acc_t[m][qi]], out=acc[m][qi][:, 0:258], lhsT=Eb[eb][:, m * 256 + qi * 128:m * 256 + (qi + 1) * 128], rhs=Vh[:, kt, :], start=(ki == 0), stop=(ki == len(kts) - 1))
                if qg + 1 < 5:
                    load_q(qg + 1)
                for qi in range(2):
                    t = (q0 + qi * 128) // 128
                    rows = slice(t * 128, (t + 1) * 128)
                    ob = uo % 2
                    uo += 1
                    P.dma("sync", zc[ob], ZC[rows, h * 256:(h + 1) * 256], [t_ZC[t]], [zc_t[ob]])
                    o0, o1 = acc[0][qi], acc[1][qi]
                    r = rr[ob]
                    P.I("vector", "reciprocal", [acc_t[0][qi]], [rr_t[ob]], out=r[:, 0:1], in_=o0[:, 256:257])
                    P.I("vector", "reciprocal", [acc_t[1][qi], rr_t[ob]], [rr_t[ob]], out=r[:, 1:2], in_=o1[:, 256:257])
                    P.I("vector", "tensor_tensor", [rr_t[ob], t_lam], [rr_t[ob]], out=r[:, 1:2], in0=r[:, 1:2], in1=lamcol, op=ALU.mult)
                    P.I("vector", "tensor_scalar", [acc_t[1][qi], rr_t[ob]], [tmp1_t[ob]], out=tmp1[ob], in0=o1[:, 0:256], scalar1=r[:, 1:2], scalar2=None, op0=ALU.mult)
                    P.I("vector", "scalar_tensor_tensor", [acc_t[0][qi], rr_t[ob], tmp1_t[ob]], [od_t[ob]], out=od[ob], in0=o0[:, 0:256], scalar=r[:, 0:1], in1=tmp1[ob], op0=ALU.mult, op1=ALU.subtract)
                    P.I("scalar", "activation", [od_t[ob]], [t_sq, rr_t[ob]], out=sq, in_=od[ob], func=AF.Square, accum_out=r[:, 2:3])
                    P.I("scalar", "activation", [rr_t[ob]], [rr_t[ob]], out=r[:, 3:4], in_=r[:, 2:3], func=AF.Sqrt, scale=1.0 / 256, bias=EPS)
                    P.I("vector", "reciprocal", [rr_t[ob]], [rr_t[ob]], out=r[:, 3:4], in_=r[:, 3:4])
                    P.I("vector", "scalar_tensor_tensor", [od_t[ob], rr_t[ob], t_subw], [od_t[ob]], out=od[ob], in0=od[ob], scalar=r[:, 3:4], in1=subw, op0=ALU.mult, op1=ALU.mult)
                    P.I("gpsimd", "tensor_tensor", [od_t[ob], zc_t[ob]], [ycb_t[ob]], out=ycb[ob], in0=od[ob], in1=zc[ob], op=ALU.mult)
                    P.dma("sync", YC[rows, h * 256:(h + 1) * 256], ycb[ob], [ycb_t[ob]], [t_YC[t]] if h == 0 else [], partial=[] if h == 0 else [t_YC[t]])
        P.barrier()
        if debug == "C":
            break

        A.reset()
        HSall = A.f32(8 * 2 * 516).rearrange("p (r d c) -> p r d c", r=8, d=2)
        t_HSall = P.tok("HSall")
        for r_ in range(8):
            P.dma("sync", HSall[:, r_, :, :], HSgat[r_ * 256:(r_ + 1) * 256, :].rearrange("(d k) c -> k d c", k=128), [t_HSgat], [t_HSall] if r_ == 0 else [], partial=[] if r_ == 0 else [t_HSall])
        Pst = A.f32(1024).rearrange("p (d c) -> p d c", d=2)
        t_Pst = P.tok("Pst")
        P.dma("sync", Pst, SCTX.rearrange("(d k) c -> k d c", k=128), [t_SCTX], [t_Pst])
        Sin = A.f32(1024).rearrange("p (d c) -> p d c", d=2)
        t_Sin = P.tok("Sin")
        Sinb = A.bf16(1024).rearrange("p (d c) -> p d c", d=2)
        t_Sinb = P.tok("Sinb")
        for d in range(2):
            order = list(range(8)) if d == 0 else list(range(7, -1, -1))
            for n_, j in enumerate(order):
                if n_ == 0:
                    P.I("vector", "tensor_scalar", [t_Pst, t_oneh], [t_Sin] if d == 0 else [], partial=[] if d == 0 else [t_Sin], out=Sin[:, d, :], in0=Pst[:, d, :], scalar1=oneh[:, j:j + 1], scalar2=None, op0=ALU.mult)
                else:
                    P.I("vector", "scalar_tensor_tensor", [t_Pst, t_oneh, t_Sin], [], partial=[t_Sin], out=Sin[:, d, :], in0=Pst[:, d, :], scalar=oneh[:, j:j + 1], in1=Sin[:, d, :], op0=ALU.mult, op1=ALU.add)
                if n_ < 7:
                    for h in range(4):
                        hs = slice(h * 128, (h + 1) * 128)
                        P.I("vector", "scalar_tensor_tensor", [t_Pst, t_HSall], [], partial=[t_Pst], out=Pst[:, d, hs], in0=Pst[:, d, hs], scalar=HSall[:, j, d, 512 + h:513 + h], in1=HSall[:, j, d, hs], op0=ALU.mult, op1=ALU.add)
        P.I("vector", "tensor_copy", [t_Sin], [t_Sinb], out=Sinb, in_=Sin)
        hnw = A.f32(128)
        t_hnw = P.tok("hnw")
        P.dma("sync", hnw, hnw_in[l, :].partition_broadcast(128), [], [t_hnw])
        t_YB = dtoks("YB", NT)
        of_ = [A.f32(512) for _ in range(2)]; of_t = [P.tok(f"of{i}") for i in range(2)]
        ob_ = [A.f32(512) for _ in range(2)]; ob_t = [P.tok(f"ob{i}") for i in range(2)]
        qhd = [[A.bf16(512) for _ in range(2)] for _ in range(2)]; qhd_t = [[P.tok(f"qhd{i}{j}") for j in range(2)] for i in range(2)]
        zb = [A.bf16(512) for _ in range(2)]; zb_t = [P.tok(f"zb{i}") for i in range(2)]
        ybb = [A.bf16(512) for _ in range(2)]; ybb_t = [P.tok(f"ybb{i}") for i in range(2)]
        sq2 = A.bf16(128); t_sq2 = P.tok("sq2")
        r4 = [A.f32(8) for _ in range(2)]; r4_t = [P.tok(f"r4{i}") for i in range(2)]
        for t in range(NT):
            b = t % 2
            rows = slice(t * 128, (t + 1) * 128)
            P.dma("sync", of_[b], OFB[0, rows, :], [t_OFB[t]], [of_t[b]])
            P.dma("sync", ob_[b], OFB[1, rows, :], [t_OFB[NT + t]], [ob_t[b]])
            P.dma("sync", zb[b], ZB[rows, :], [t_ZB[t]], [zb_t[b]])
            P.I("gpsimd", "tensor_tensor", [of_t[b], ob_t[b]], [of_t[b]], out=of_[b], in0=of_[b], in1=ob_[b], op=ALU.add)
            if t < 8:
                pX = ps[t % 2]
                pXt = ps_t[t % 2]
                for d in range(2):
                    P.dma("sync", qhd[b][d], QHAT[(d * 8 + t) * 128:(d * 8 + t + 1) * 128, :], [t_QHAT[d * 8 + t]], [qhd_t[b][d]])
                n_ = 0
                for h in range(4):
                    hs = slice(h * 128, (h + 1) * 128)
                    for d in range(2):
                        P.I("tensor", "matmul", [qhd_t[b][d], t_Sinb], [pXt] if n_ == 0 else [], partial=[] if n_ == 0 else [pXt], out=pX[:, hs], lhsT=qhd[b][d][:, hs], rhs=Sinb[:, d, hs], start=(n_ == 0), stop=(n_ == 7))
                        n_ += 1
                P.I("vector", "tensor_tensor", [pXt, of_t[b]], [of_t[b]], out=of_[b], in0=pX[:, :], in1=of_[b], op=ALU.add)
            for h in range(4):
                hs = slice(h * 128, (h + 1) * 128)
                P.I("scalar", "activation", [of_t[b]], [t_sq2, r4_t[b]] if h == 0 else [t_sq2], partial=[] if h == 0 else [r4_t[b]], out=sq2, in_=of_[b][:, hs], func=AF.Square, accum_out=r4[b][:, h:h + 1])
            P.I("scalar", "activation", [r4_t[b]], [r4_t[b]], out=r4[b][:, 4:8], in_=r4[b][:, 0:4], func=AF.Sqrt, scale=1.0 / 128, bias=EPS)
            P.I("vector", "reciprocal", [r4_t[b]], [r4_t[b]], out=r4[b][:, 4:8], in_=r4[b][:, 4:8])
            for h in range(4):
                hs = slice(h * 128, (h + 1) * 128)
                P.I("vector", "scalar_tensor_tensor", [of_t[b], r4_t[b], t_hnw], [], partial=[of_t[b]], out=of_[b][:, hs], in0=of_[b][:, hs], scalar=r4[b][:, 4 + h:5 + h], in1=hnw, op0=ALU.mult, op1=ALU.mult)
            P.I("gpsimd", "tensor_tensor", [of_t[b], zb_t[b]], [ybb_t[b]], out=ybb[b], in0=of_[b], in1=zb[b], op=ALU.mult)
            P.dma("sync", YB[rows, :], ybb[b], [ybb_t[b]], [t_YB[t]])
        P.barrier()
        if debug == "D":
            break

        A.reset()
        last = (l == DEPTH - 1)
        tiles_e = list(range(8)) if last else list(range(NT))
        yT = A.bf16(16 * TOK).rearrange("p (k t) -> p k t", t=TOK)
        t_yT = [P.tok(f"yT{t}") for t in range(NT)]
        Wo = [A.bf16(16 * 512).rearrange("p (k n) -> p k n", n=512) for _ in range(2)]
        Wo_t = [P.tok(f"Wo{i}") for i in range(2)]
        ytile = [A.bf16(D) for _ in range(2)]
        ytile_t = [P.tok(f"ytile{i}") for i in range(2)]
        Gt = [A.f32(D) for _ in range(2)]
        Gt_t = [P.tok(f"Gt{i}") for i in range(2)]
        xin = [A.f32(512) for _ in range(3)]
        xin_t = [P.tok(f"xin{i}") for i in range(3)]
        xo = [A.f32(512) for _ in range(3)]
        xo_t = [P.tok(f"xo{i}") for i in range(3)]
        g_XS = P.grp("XSw")
        t_XS = [P.tok(f"XS{t}", g_XS) for t in range(NT)]
        for s_ in range(2):
            P.dma("sync", Gt[s_].rearrange("p (r c) -> p r c", c=256), modvec(l, s_, 2).partition_broadcast(128), [t_modgat], [Gt_t[s_]])
        for t in tiles_e:
            b = t % 2
            rows = slice(t * 128, (t + 1) * 128)
            P.dma("sync", ytile[b][:, 0:512], YA[rows, :], [t_YA[t]], [ytile_t[b]])
            P.dma("sync", ytile[b][:, 512:1024], YB[rows, :], [t_YB[t]], [], partial=[ytile_t[b]])
            P.dma("sync", ytile[b][:, 1024:2048], YC[rows, :], [t_YC[t]], [], partial=[ytile_t[b]])
            for half in range(2):
                pt = psT[half]
                for j in range(8):
                    kc = half * 8 + j
                    P.I("tensor", "transpose", [ytile_t[b], t_const], [psT_t[half]] if j == 0 else [], partial=[] if j == 0 else [psT_t[half]], out=pt[:, j * 128:(j + 1) * 128], in_=ytile[b][:, kc * 128:(kc + 1) * 128], identity=ident)
                dsth = yT[:, half * 8:(half + 1) * 8, t * 128:(t + 1) * 128]
                srch = pt[:, :].rearrange("p (k t) -> p k t", t=128)
                if half == 0:
                    P.I("vector", "tensor_copy", [psT_t[half]], [t_yT[t]], out=dsth, in_=srch)
                else:
                    P.I("scalar", "copy", [psT_t[half]], [], partial=[t_yT[t]], out=dsth, in_=srch)
        ux = 0
        wq = 0
        def load_wo(gi_):
            for q4 in range(4):
                P.dma("gpsimd", Wo[gi_ % 2][:, q4 * 4:(q4 + 1) * 4, :], wout_in[l, q4 * 512:(q4 + 1) * 512, gi_ * 512:(gi_ + 1) * 512].rearrange("(k p) n -> p k n", p=128), [], [Wo_t[gi_ % 2]] if q4 == 0 else [], partial=[] if q4 == 0 else [Wo_t[gi_ % 2]])
        load_wo(0)
        for gi in range(4):
            wb = Wo[gi % 2]
            wt = Wo_t[gi % 2]
            for t in tiles_e:
                if t == 5 and gi + 1 < 4:
                    load_wo(gi + 1)
                pb = wq % 4
                wq += 1
                pp = ps[pb]
                rows = slice(t * 128, (t + 1) * 128)
                cols = slice(gi * 512, (gi + 1) * 512)
                for kc in range(16):
                    P.I("tensor", "matmul", [wt, t_yT[t]], [ps_t[pb]] if kc == 0 else [], partial=[] if kc == 0 else [ps_t[pb]], out=pp[:, :], lhsT=yT[:, kc, t * 128:(t + 1) * 128], rhs=wb[:, kc, :], start=(kc == 0), stop=(kc == 15))
                xb = ux % 3
                ux += 1
                P.dma("sync", xin[xb], xrows(t)[:, cols], [], [xin_t[xb]])
                sg = 0 if t < 8 else 1
                P.I("vector", "tensor_tensor", [ps_t[pb], Gt_t[sg]], [xo_t[xb]], out=xo[xb], in0=pp[:, :], in1=Gt[sg][:, cols], op=ALU.mult)
                P.I("gpsimd", "tensor_tensor", [xo_t[xb], xin_t[xb]], [xo_t[xb]], out=xo[xb], in0=xo[xb], in1=xin[xb], op=ALU.add)
                P.dma("sync", xs_dst[rows, cols], xo[xb], [xo_t[xb]], [t_XS[t]] if gi == 0 else [], partial=[] if gi == 0 else [t_XS[t]])
        P.barrier()
        if not (last or l == n_layers - 1):
            P.new_epoch()
        if last or l == n_layers - 1:
            A.reset()
            fnw = A.f32(D)
            t_fnw = P.tok("fnw")
            P.dma("sync", fnw, fnw_in.partition_broadcast(128), [], [t_fnw])
            xf = [A.f32(D) for _ in range(2)]
            xf_t = [P.tok(f"xf{i}") for i in range(2)]
            jk = A.bf16(D)
            t_jk = P.tok("jk")
            r5 = [A.f32(4) for _ in range(2)]
            r5_t = [P.tok(f"r5{i}") for i in range(2)]
            t_out = P.tok("out")
            for t in range(8):
                b = t % 2
                rows = slice(t * 128, (t + 1) * 128)
                P.dma("sync", xf[b], xs_dst[rows, :], [t_XS[t]], [xf_t[b]])
                P.I("scalar", "activation", [xf_t[b]], [t_jk, r5_t[b]], out=jk, in_=xf[b], func=AF.Square, accum_out=r5[b][:, 0:1])
                P.I("scalar", "activation", [r5_t[b]], [r5_t[b]], out=r5[b][:, 1:2], in_=r5[b][:, 0:1], func=AF.Sqrt, scale=1.0 / D, bias=EPS)
                P.I("vector", "reciprocal", [r5_t[b]], [r5_t[b]], out=r5[b][:, 1:2], in_=r5[b][:, 1:2])
                P.I("vector", "scalar_tensor_tensor", [xf_t[b], r5_t[b], t_fnw], [xf_t[b]], out=xf[b], in0=xf[b], scalar=r5[b][:, 1:2], in1=fnw, op0=ALU.mult, op1=ALU.mult)
                P.dma("sync", out_ap[rows, :], xf[b], [xf_t[b]], [], partial=[t_out])
            P.barrier()

    es_close = es
    if debug == "A":
        fin = P.tok("fin")
    P.barrier()
    global _LAST_P
    _LAST_P = P
    print("ops", len(P.ops), "semgroups", len(P.grps))
    P.emit()
    es.close()
    return nc


def prep_inputs(inp, nlw=DEPTH):
    f = lambda a: np.ascontiguousarray(np.asarray(a, dtype=np.float32))
    x = f(inp["x"])[0]; ctx = f(inp["ctx"])[0]; c = f(inp["c"])[0]; c_ctx = f(inp["c_ctx"])
    ada_w = f(inp["ada_w"]); ada_b = f(inp["ada_b"])
    cc = np.stack([c.reshape(16, 128).T, c_ctx.reshape(16, 128).T], axis=-1).reshape(128, 32)
    ws = f(inp["gmlp_ws"])
    wsT = np.ascontiguousarray(ws.transpose(0, 3, 1, 2)).reshape(DEPTH, 128, 512)
    bs = f(inp["gmlp_bs"])
    gbs = np.ascontiguousarray(bs.transpose(2, 0, 1)).reshape(128, DEPTH * 4)
    s_ = np.arange(128)[:, None]; t_ = np.arange(128)[None, :]
    same = (s_ // 32 == t_ // 32)
    mid_f = (t_ // 32) * 32 + 15
    mid_b = (t_ // 32) * 32 + 16
    Mq_f = same * ((s_ <= t_).astype(np.float32) - (s_ <= mid_f)); Md_f = same * (s_ > t_); mk_f = same * (s_ <= t_)
    Mq_b = same * ((s_ >= t_).astype(np.float32) - (s_ >= mid_b)); Md_b = same * (s_ < t_); mk_b = same * (s_ >= t_)
    perm = np.zeros((128, 128), np.float32)
    for do in range(128):
        if (do % 64) < 32: perm[do + 32, do] = -1.0
        else: perm[do - 32, do] = 1.0
    sel = np.zeros((128, 128), np.float32)
    sv = np.arange(128)
    for c_ in range(4):
        inch = (sv // 32 == c_)
        sel[:, c_] = inch & (sv % 32 <= 15); sel[:, 4 + c_] = inch
        sel[:, 8 + c_] = inch & (sv % 32 >= 16); sel[:, 12 + c_] = inch
    cmat = np.concatenate([np.eye(128, dtype=np.float32), Mq_f, Md_f, mk_f, Mq_b, Md_b, mk_b, perm, sel], axis=1).astype(np.float32)
    inv = (10000.0 ** (-np.arange(0, 64, 2, dtype=np.float32) / 64)).astype(np.float32)
    common = dict(ctx=ctx, cc=cc, norm_w=f(inp["norm_w"]), w_in=f(inp["w_in"]), w_out=f(inp["w_out"]),
                  gmlp_ln_w=f(inp["gmlp_ln_w"]), gmlp_ln_b=f(inp["gmlp_ln_b"]), gmlp_wsT=wsT, gmlp_bs=gbs,
                  hgrn_lb=f(inp["hgrn_lower_bounds"]).reshape(-1), hgrn_norm_w=f(inp["hgrn_norm_w"]),
                  diff_lambda=f(inp["diff_lambda"]).reshape(DEPTH, 512), diff_subln_w=f(inp["diff_subln_w"]),
                  final_norm_w=f(inp["final_norm_w"]), cmat=cmat)
    maps = []
    for i in range(NCORES):
        m = dict(common)
        m["x"] = np.ascontiguousarray(x[i * LAT:(i + 1) * LAT])
        m["ada_w"] = np.ascontiguousarray(np.concatenate([ada_w[:, :, b * D + i * 256: b * D + (i + 1) * 256] for b in range(3)], axis=2))
        m["ada_b"] = np.ascontiguousarray(np.concatenate([ada_b[:, b * D + i * 256: b * D + (i + 1) * 256] for b in range(3)], axis=1)).reshape(-1)
        pos = np.arange(i * LAT, (i + 1) * LAT)
        rowp = (pos // 64).astype(np.float32); colp = (pos % 64).astype(np.float32)
        d = np.arange(128)
        fr = inv[d % 32]
        ang = np.where((d < 64)[:, None], rowp[None, :] * fr[:, None], colp[None, :] * fr[:, None]).astype(np.float32)
        m["rope"] = np.concatenate([np.cos(ang), np.sin(ang)], axis=1).astype(np.float32)
        oh = np.zeros((128, 8), np.float32); oh[:, i] = 1.0
        m["onehot"] = oh
        maps.append(m)
    if nlw != DEPTH:
        for m in maps:
            m["ada_w"] = np.ascontiguousarray(m["ada_w"][:nlw]); m["ada_b"] = np.ascontiguousarray(m["ada_b"].reshape(DEPTH, 768)[:nlw]).reshape(-1)
            m["w_in"] = np.ascontiguousarray(m["w_in"][:nlw]); m["w_out"] = np.ascontiguousarray(m["w_out"][:nlw])
    return maps


_NC_CACHE = {}


def kernel(**inputs):
    maps = prep_inputs(inputs)
    if "nc" not in _NC_CACHE:
        _NC_CACHE["nc"] = build()
    res = run_bass_kernel_spmd(_NC_CACHE["nc"], maps, core_ids=list(range(NCORES)))
    out = np.concatenate([np.asarray(r["out"]) for r in res.results], axis=0)
    return out.reshape(1, NCORES * LAT, D).astype(np.float32)
```
